# Optimizing a Trainium2 kernel written in Bass

```python
import jax, jax.numpy as jnp
from jax import lax
import numpy as np

D_MODEL = 1024
BATCH = 8
SEQ = 2048
DEPTH = 2
DEC_BATCH = 128
DEC_SEQ = 4
PAST_LEN = 16384
PAGE_SIZE = 128

N_MIXERS = 2
N_CONV_LAYERS = (DEPTH + 1) // 2
N_SGU_LAYERS = DEPTH // 2
CONV_WIDTH = 31
D_SGU = D_MODEL
SGU_HEADS = 8
SGU_HEAD_DIM = D_SGU // SGU_HEADS
SGU_CHUNK = 128
D_FF = ((8 * D_MODEL // 3 + 255) // 256) * 256
RMS_EPS = 1e-6
LN_EPS = 1e-5

kernel_name = "conformer_conv_gmlp_hybrid_step"


def _rmsnorm(x, g):
    xf = x.astype(jnp.float32)
    y = xf * lax.rsqrt(jnp.mean(xf * xf, axis=-1, keepdims=True) + RMS_EPS)
    return (y * g.astype(jnp.float32)).astype(x.dtype)


def _layernorm(x, g, b):
    xf = x.astype(jnp.float32)
    mu = jnp.mean(xf, axis=-1, keepdims=True)
    var = jnp.mean(jnp.square(xf - mu), axis=-1, keepdims=True)
    y = (xf - mu) * lax.rsqrt(var + LN_EPS)
    return (y * g.astype(jnp.float32) + b.astype(jnp.float32)).astype(x.dtype)


def _conv_mixer(h, ctx, w_pw1, b_pw1, w_dw, b_dw, ln_g, ln_b, w_pw2, b_pw2):
    a = h @ w_pw1 + b_pw1
    glu = a[..., :D_MODEL] * jax.nn.sigmoid(a[..., D_MODEL:])
    full = jnp.concatenate([ctx.astype(glu.dtype), glu], axis=1)
    conv = lax.conv_general_dilated(
        full, w_dw[:, None, :].astype(full.dtype), window_strides=(1,), padding='VALID',
        dimension_numbers=('NWC', 'WIO', 'NWC'), feature_group_count=D_MODEL) + b_dw
    y = jax.nn.silu(_layernorm(conv, ln_g, ln_b))
    out = y @ w_pw2 + b_pw2
    return out, full[:, -(CONV_WIDTH - 1):]


def _sgu_mixer(h, w_in, b_in, ln_g, ln_b, w_s, b_s, w_out, b_out):
    B, L, _ = h.shape
    z = jax.nn.gelu(h @ w_in + b_in, approximate=False)
    u, v = z[..., :D_SGU], z[..., D_SGU:]
    v = _layernorm(v, ln_g, ln_b)
    c = min(L, SGU_CHUNK)
    n = L // c
    vc = v.reshape(B, n, c, SGU_HEADS, SGU_HEAD_DIM)
    mask = jnp.tril(jnp.ones((c, c), dtype=w_s.dtype))
    w = w_s[:, :c, :c] * mask
    mixed = jnp.einsum('hts,bnshd->bnthd', w, vc) + jnp.transpose(b_s[:, :c])[:, :, None]
    out = (u * mixed.reshape(B, L, D_SGU)) @ w_out + b_out
    return out, v


def _swiglu(h, w_gate, w_up, w_down):
    return (jax.nn.silu(h @ w_gate) * (h @ w_up)) @ w_down


def setup_inputs(seed: int = 0) -> dict:
    key = jax.random.key(seed)
    ks = iter(jax.random.split(key, 32))

    def nrm(shape, scale):
        return jax.random.normal(next(ks), shape, jnp.float32) * scale

    NC, NS = N_CONV_LAYERS, N_SGU_LAYERS
    return {
        "x_prompt": nrm((BATCH, SEQ, D_MODEL), 1.0),
        "x_sample": nrm((DEC_BATCH, DEC_SEQ, D_MODEL), 1.0),
        "state_conv": nrm((NC, DEC_BATCH, CONV_WIDTH - 1, D_MODEL), 0.5),
        "conv_norm_g": 1.0 + nrm((NC, D_MODEL), 0.02),
        "conv_w_pw1": nrm((NC, D_MODEL, 2 * D_MODEL), D_MODEL ** -0.5),
        "conv_b_pw1": nrm((NC, 2 * D_MODEL), 0.02),
        "conv_w_dw": nrm((NC, CONV_WIDTH, D_MODEL), CONV_WIDTH ** -0.5),
        "conv_b_dw": nrm((NC, D_MODEL), 0.02),
        "conv_ln_g": 1.0 + nrm((NC, D_MODEL), 0.02),
        "conv_ln_b": nrm((NC, D_MODEL), 0.02),
        "conv_w_pw2": nrm((NC, D_MODEL, D_MODEL), D_MODEL ** -0.5),
        "conv_b_pw2": nrm((NC, D_MODEL), 0.02),
        "sgu_norm_g": 1.0 + nrm((NS, D_MODEL), 0.02),
        "sgu_w_in": nrm((NS, D_MODEL, 2 * D_SGU), D_MODEL ** -0.5),
        "sgu_b_in": nrm((NS, 2 * D_SGU), 0.02),
        "sgu_ln_g": 1.0 + nrm((NS, D_SGU), 0.02),
        "sgu_ln_b": nrm((NS, D_SGU), 0.02),
        "sgu_w_s": nrm((NS, SGU_HEADS, SGU_CHUNK, SGU_CHUNK), 0.5 * SGU_CHUNK ** -0.5),
        "sgu_b_s": 1.0 + nrm((NS, SGU_HEADS, SGU_CHUNK), 0.01),
        "sgu_w_out": nrm((NS, D_SGU, D_MODEL), D_SGU ** -0.5),
        "sgu_b_out": nrm((NS, D_MODEL), 0.02),
        "ffn_norm_g": 1.0 + nrm((DEPTH, D_MODEL), 0.02),
        "ffn_w_gate": nrm((DEPTH, D_MODEL, D_FF), D_MODEL ** -0.5),
        "ffn_w_up": nrm((DEPTH, D_MODEL, D_FF), D_MODEL ** -0.5),
        "ffn_w_down": nrm((DEPTH, D_FF, D_MODEL), D_FF ** -0.5),
        "final_norm_g": 1.0 + nrm((D_MODEL,), 0.02),
    }


def reference(x_prompt, x_sample, state_conv,
              conv_norm_g, conv_w_pw1, conv_b_pw1, conv_w_dw, conv_b_dw, conv_ln_g, conv_ln_b,
              conv_w_pw2, conv_b_pw2,
              sgu_norm_g, sgu_w_in, sgu_b_in, sgu_ln_g, sgu_ln_b, sgu_w_s, sgu_b_s, sgu_w_out, sgu_b_out,
              ffn_norm_g, ffn_w_gate, ffn_w_up, ffn_w_down, final_norm_g):
    xp, xs = x_prompt, x_sample
    conv_p, conv_s, v_s = [], [], []
    for i in range(DEPTH):
        j = i // N_MIXERS
        if i % N_MIXERS == 0:
            params = (conv_w_pw1[j], conv_b_pw1[j], conv_w_dw[j], conv_b_dw[j],
                      conv_ln_g[j], conv_ln_b[j], conv_w_pw2[j], conv_b_pw2[j])
            ctx_p = jnp.zeros((xp.shape[0], CONV_WIDTH - 1, D_MODEL), xp.dtype)
            op, cp = _conv_mixer(_rmsnorm(xp, conv_norm_g[j]), ctx_p, *params)
            os_, cs = _conv_mixer(_rmsnorm(xs, conv_norm_g[j]), state_conv[j], *params)
            conv_p.append(cp)
            conv_s.append(cs)
        else:
            params = (sgu_w_in[j], sgu_b_in[j], sgu_ln_g[j], sgu_ln_b[j],
                      sgu_w_s[j], sgu_b_s[j], sgu_w_out[j], sgu_b_out[j])
            op, _ = _sgu_mixer(_rmsnorm(xp, sgu_norm_g[j]), *params)
            os_, vs = _sgu_mixer(_rmsnorm(xs, sgu_norm_g[j]), *params)
            v_s.append(vs)
        xp = xp + op
        xs = xs + os_
        xp = xp + _swiglu(_rmsnorm(xp, ffn_norm_g[i]), ffn_w_gate[i], ffn_w_up[i], ffn_w_down[i])
        xs = xs + _swiglu(_rmsnorm(xs, ffn_norm_g[i]), ffn_w_gate[i], ffn_w_up[i], ffn_w_down[i])
    y_prompt = _rmsnorm(xp, final_norm_g)
    y_sample = _rmsnorm(xs, final_norm_g)
    new_conv_prompt = jnp.stack(conv_p)
    new_conv_sample = jnp.stack(conv_s)
    new_sgu_v_sample = jnp.stack(v_s)
    return (y_prompt, y_sample, new_conv_prompt, new_conv_sample, new_sgu_v_sample)
```

```python
import contextlib
import numpy as np
import concourse.bass as bass
import concourse.mybir as mybir
from concourse.bass_utils import run_bass_kernel_spmd

F32 = mybir.dt.float32
BF16 = mybir.dt.bfloat16
ALU = mybir.AluOpType
AF = mybir.ActivationFunctionType

NCORES = 8
D = 1024
DFF = 2816
TT = 2112
TILES = [(0, 512), (512, 512), (1024, 512), (1536, 512), (2048, 64)]
NT = len(TILES)
KW = 31
RMS_EPS = 1e-6
LN_EPS = 1e-5
USE_TANH = False
BIAS_K128 = True
GSC = 0.5 if USE_TANH else 1.0
PER_CHUNK_G = True
DELAY_STATS = True
SPLIT_NORM = True
USE_WSCRATCH = True

V_CNG, V_B1A, V_B1G, V_BDW, V_LNG, V_LNB, V_BP2 = 0, 8, 16, 24, 32, 40, 48
V_SNG, V_BIU, V_BOUT, V_FNG0, V_FNG1, V_FIN, V_WDW = 56, 64, 72, 80, 88, 96, 104
NV = V_WDW + 8 * KW
FFN_SLICES = [(2560, 256)] + [(512 * s, 512) for s in range(5)]


import bisect


class Res:
    __slots__ = ("w", "r")

    def __init__(self, init=None):
        self.w = None
        self.r = dict(init) if init else {}


def _tkey(tok):
    return id(tok[1])


def _tmax(a, b):
    return a if a[2] >= b[2] else b


class Q:
    def __init__(self, nc, eng, name):
        self.e = eng
        self.sem = nc.alloc_semaphore(name)
        self.instrs = []
        self.inc_pos = []
        self.seen_e = {}
        self.seen_d = {}

    def record(self, ins):
        self.instrs.append(ins)
        return ("E", self, len(self.instrs) - 1)

    def value_for(self, pos):
        if not self.inc_pos or pos > self.inc_pos[-1]:
            self.instrs[pos].then_inc(self.sem, 1)
            self.inc_pos.append(pos)
            return len(self.inc_pos), pos
        k = bisect.bisect_left(self.inc_pos, pos)
        return k + 1, self.inc_pos[k]

    def wait(self, tok):
        if tok[0] == "E":
            owner, pos = tok[1], tok[2]
            if self.seen_e.get(id(owner), -1) >= pos:
                return
            val, zpos = owner.value_for(pos)
            self.e.wait_ge(owner.sem, val)
            self.seen_e[id(owner)] = zpos
        else:
            sem, val = tok[1], tok[2]
            if self.seen_d.get(id(sem), 0) >= val:
                return
            self.e.wait_ge(sem, val)
            self.seen_d[id(sem)] = val


def _deps(q, reads, writes):
    need = {}
    for r in reads:
        if r.w is not None:
            k = _tkey(r.w)
            need[k] = _tmax(need[k], r.w) if k in need else r.w
    for w in writes:
        toks = list(w.r.values())
        if w.w is not None:
            toks.append(w.w)
        for tok in toks:
            k = _tkey(tok)
            need[k] = _tmax(need[k], tok) if k in need else tok
    for tok in need.values():
        q.wait(tok)


def _commit(tok, reads, writes):
    k = _tkey(tok)
    for r in reads:
        old = r.r.get(k)
        if old is None or old[2] < tok[2]:
            r.r[k] = tok
    for w in writes:
        w.w = tok
        w.r = {}


def build():
    nc = bass.Bass("TRN2", target_bir_lowering=False)

    def din(name, shape):
        return nc.dram_tensor(name, list(shape), F32, kind="ExternalInput").ap()

    def dout(name, shape):
        return nc.dram_tensor(name, list(shape), F32, kind="ExternalOutput").ap()

    xp = din("xp", [2048, D])
    xs = din("xs", [64, D])
    sc = din("sc", [480, D])
    vecs_d = din("vecs", [128, NV])
    rows_d = din("rows", [3, D])
    bs_d = din("bs", [1, D])
    wsT_d = din("wsT", [128, 8, 128])
    w_pw1 = din("w_pw1", [D, 2 * D])
    w_pw2 = din("w_pw2", [D, D])
    w_in = din("w_in", [D, 2 * D])
    w_out = din("w_out", [D, D])
    w_gate = din("w_gate", [2, D, DFF])
    w_up = din("w_up", [2, D, DFF])
    w_down = din("w_down", [2, DFF, D])
    y_p = dout("y_p", [2048, D])
    y_s = dout("y_s", [64, D])
    ncp = dout("ncp", [30, D])
    ncs = dout("ncs", [16, 30, D])
    nv_o = dout("nv", [64, D])
    wsc = {
        "w_pw1": nc.dram_tensor("wsc_pw1", [D, 2 * D], BF16, kind="Internal").ap(),
        "w_pw2": nc.dram_tensor("wsc_pw2", [D, D], BF16, kind="Internal").ap(),
        "w_in": nc.dram_tensor("wsc_in", [D, 2 * D], BF16, kind="Internal").ap(),
        "w_out": nc.dram_tensor("wsc_out", [D, D], BF16, kind="Internal").ap(),
    }
    wname = {id(w_pw1): "w_pw1", id(w_pw2): "w_pw2", id(w_in): "w_in", id(w_out): "w_out"}

    PE = Q(nc, nc.tensor, "s_pe")
    ACT = Q(nc, nc.scalar, "s_act")
    DVE = Q(nc, nc.vector, "s_dve")
    POOL = Q(nc, nc.gpsimd, "s_pool")
    SP = Q(nc, nc.sync, "s_sp")

    def op(q, fn, reads=(), writes=()):
        _deps(q, reads, writes)
        tok = q.record(fn())
        _commit(tok, reads, writes)
        return tok

    def pe_multi(fns, reads=(), writes=()):
        _deps(PE, reads, writes)
        ins = None
        for f in fns:
            ins = f()
        tok = PE.record(ins)
        _commit(tok, reads, writes)
        return tok

    def mmgroup(mms, reads=(), writes=()):
        n = len(mms)
        fns = []
        for i, (o, l, r) in enumerate(mms):
            fns.append(lambda o=o, l=l, r=r, i=i: nc.tensor.matmul(o, l, r, start=(i == 0), stop=(i == n - 1)))
        return pe_multi(fns, reads, writes)

    NDS = 16
    dsems = [[nc.alloc_semaphore(f"s_dma{i}"), 0] for i in range(NDS)]
    dstate = {"i": 0}
    out_toks = []

    def dma(q, out, in_, reads=(), writes=(), is_out=False):
        _deps(q, reads, writes)
        ent = dsems[dstate["i"] % NDS]
        dstate["i"] += 1
        if ent[1] > 0:
            q.wait(("D", ent[0], ent[1]))
        ins = q.e.dma_start(out=out, in_=in_)
        ent[1] += 16
        ins.then_inc(ent[0], 16)
        tok = ("D", ent[0], ent[1])
        _commit(tok, reads, writes)
        if is_out:
            out_toks.append(tok)
        return tok

    def fence(res_list):
        acc = {}
        for r in res_list:
            toks = list(r.r.values())
            if r.w is not None:
                toks.append(r.w)
            for tok in toks:
                k = _tkey(tok)
                if k not in acc or acc[k][2] < tok[2]:
                    acc[k] = tok
        return acc

    es = contextlib.ExitStack()
    with es:
        def sb(name, shape, dt):
            return es.enter_context(nc.sbuf_tensor(name, list(shape), dt))

        XT = sb("XT", [128, 8, TT], F32)
        HT = sb("HT", [128, 8, TT], BF16)
        NRING = 4
        RING = [sb(f"ring{i}", [128, 4096], BF16) for i in range(NRING)]
        RSTD = sb("RSTD", [128, 512], F32)
        VEC = sb("VEC", [128, NV], F32)
        IDENT = sb("IDENT", [128, 128], F32)
        ONESF = sb("ONESF", [128, 128], F32)
        IDB = sb("IDB", [128, 128], BF16)
        ONESM = sb("ONESM", [128, 128], BF16)
        KB = 128 if BIAS_K128 else 2
        ONES2 = sb("ONES2", [KB, 128], BF16)
        HB = sb("HB", [128, 8], F32)
        HBr = Res()
        PS = [es.enter_context(nc.psum_tensor(f"ps{i}", [128, 512], F32)) for i in range(8)]

        XTr = [[Res() for _ in range(NT)] for _ in range(8)]
        HTr = [[Res() for _ in range(NT)] for _ in range(8)]
        RINGr = [Res() for _ in range(NRING)]
        SQr, RSTDr, VECr, IDENTr, ONESFr, IDBr, ONESMr, ONES2r = (Res() for _ in range(8))
        OUTTr = []
        OUTTs = []
        PSr = [Res() for _ in range(8)]
        ring_sems = [[nc.alloc_semaphore(f"s_ring{i}"), 0] for i in range(NRING)]
        ring_sems_hw = [[nc.alloc_semaphore(f"s_ringhw{i}"), 0] for i in range(NRING)]

        pstate = {"i": 0, "all": False}

        def bank():
            i = (pstate["i"] % 8) if pstate["all"] else (2 + pstate["i"] % 6)
            pstate["i"] += 1
            return PS[i], PSr[i]

        def rsqrt_eps(out_ap, out_res, in_ap, in_reads, eps):
            op(ACT, lambda: nc.scalar.activation(out=out_ap, in_=in_ap, func=AF.Sqrt, bias=EPS_AP[eps][:out_ap.shape[0], :],
                                                 scale=1.0), tuple(in_reads) + (EPSr,), (out_res,))
            op(DVE, lambda: nc.vector.reciprocal(out=out_ap, in_=out_ap), (), (out_res,))

        def vcol(c0, c=0):
            return VEC[:, c0 + c:c0 + c + 1]

        sched = []
        for t in range(NT - 1):
            sched += [("col", w_pw1, None, 0), ("col", w_pw1, None, 1024)]
            if t > 0:
                sched += [("col", w_pw2, None, 0), ("col", w_pw2, None, 512)]
            sched += [("col", w_pw1, None, 512), ("col", w_pw1, None, 1536)]
        sched += [("col", w_pw2, None, 0), ("col", w_pw2, None, 512)]

        def ffn_sched(l):
            out = []
            for (c0, ncol) in FFN_SLICES:
                out += [("col", w_gate, l, c0, ncol), ("col", w_up, l, c0, ncol),
                        ("row", w_down, l, c0 // 128, ncol // 128)]
            return out

        sched += ffn_sched(0)
        M1_GROUPS = [[0], [1], [2], [3, 4]]
        for _g in M1_GROUPS:
            sched += [("col", w_in, None, 1024), ("col", w_in, None, 1536), ("col", w_in, None, 0),
                      ("col", w_in, None, 512), ("col", w_out, None, 0), ("col", w_out, None, 512)]
        sched += ffn_sched(1)
        wst = {"issued": 0, "next": 0}

        wsc_res = {}

        def issue_block(i):
            d = sched[i]
            slot = i % NRING
            if d[0] == "col" and d[2] is None and USE_WSCRATCH:
                w, c0 = d[1], d[3]
                key = (wname[id(w)], c0)
                dst = RING[slot][:, 0:4096].rearrange("p (k n) -> p k n", k=8)
                sc_ap = wsc[key[0]][:, c0:c0 + 512].rearrange("(k p) n -> p k n", p=128)
                r = RINGr[slot]
                ent = ring_sems[slot]
                if key not in wsc_res:
                    wsc_res[key] = Res()
                    _deps(POOL, (), (r,))
                    ins = nc.gpsimd.dma_start(out=dst, in_=w[:, c0:c0 + 512].rearrange("(k p) n -> p k n", p=128))
                    ent[1] += 16
                    ins.then_inc(ent[0], 16)
                    _commit(("D", ent[0], ent[1]), (), (r,))
                    dma(SP, sc_ap, dst, (r,), (wsc_res[key],))
                else:
                    ent = ring_sems_hw[slot]
                    _deps(SP, (wsc_res[key],), (r,))
                    ins = nc.sync.dma_start(out=dst, in_=sc_ap)
                    ent[1] += 16
                    ins.then_inc(ent[0], 16)
                    _commit(("D", ent[0], ent[1]), (wsc_res[key],), (r,))
                return
            if d[0] == "col":
                w, l, c0 = d[1], d[2], d[3]
                ncol = d[4] if len(d) > 4 else 512
                src = (w if l is None else w[l])[:, c0:c0 + ncol].rearrange("(k p) n -> p k n", p=128)
                dst = RING[slot][:, 0:8 * ncol].rearrange("p (k n) -> p k n", k=8)
            else:
                w, l, j0, nj = d[1], d[2], d[3], d[4]
                src = w[l][j0 * 128:(j0 + nj) * 128, :].rearrange("(j p) n -> p j n", p=128)
                dst = RING[slot][:, 0:nj * 1024].rearrange("p (j n) -> p j n", j=nj)
            r = RINGr[slot]
            _deps(POOL, (), (r,))
            ent = ring_sems[slot]
            ins = nc.gpsimd.dma_start(out=dst, in_=src)
            ent[1] += 16
            ins.then_inc(ent[0], 16)
            _commit(("D", ent[0], ent[1]), (), (r,))

        def next_block():
            i = wst["next"]
            wst["next"] += 1
            while wst["issued"] < min(len(sched), i + NRING - 1):
                issue_block(wst["issued"])
                wst["issued"] += 1
            d = sched[i]
            slot = i % NRING
            if d[0] == "col":
                ncol = d[4] if len(d) > 4 else 512
                view = RING[slot][:, 0:8 * ncol].rearrange("p (k n) -> p k n", k=8)
            else:
                nj = d[4]
                view = RING[slot][:, 0:nj * 1024].rearrange("p (j n) -> p j n", j=nj)
            return view, RINGr[slot]

        op(POOL, lambda: nc.gpsimd.memset(ONESF[:], 1.0), (), (ONESFr,))
        op(POOL, lambda: nc.gpsimd.affine_select(out=IDENT[:], in_=ONESF[:], pattern=[[1, 128]],
                                                 compare_op=ALU.is_equal, fill=0.0, base=0,
                                                 channel_multiplier=-1), (ONESFr,), (IDENTr,))
        op(POOL, lambda: nc.gpsimd.tensor_copy(out=IDB[:], in_=IDENT[:]), (IDENTr,), (IDBr,))
        TRI = sb("TRI", [128, 128], F32)
        TRIr = Res()
        op(POOL, lambda: nc.gpsimd.affine_select(out=TRI[:], in_=ONESF[:], pattern=[[1, 128]],
                                                 compare_op=ALU.is_ge, fill=0.0, base=0,
                                                 channel_multiplier=-1), (ONESFr,), (TRIr,))
        op(POOL, lambda: nc.gpsimd.memset(ONESM[:], 1.0 / 1024.0), (), (ONESMr,))
        op(POOL, lambda: nc.gpsimd.memset(ONES2[:], 1.0), (), (ONES2r,))
        dma(SP, VEC[:], vecs_d[:, :], (), (VECr,))
        op(DVE, lambda: nc.vector.tensor_scalar_mul(out=HB[:], in0=VEC[:, V_B1G:V_B1G + 8], scalar1=0.5), (VECr,), (HBr,))
        EPST = sb("EPST", [128, 2], F32)
        EPSr = Res()
        op(POOL, lambda: nc.gpsimd.memset(EPST[:, 0:1], RMS_EPS), (), (EPSr,))
        op(POOL, lambda: nc.gpsimd.memset(EPST[:, 1:2], LN_EPS), (), (EPSr,))
        EPS_AP = {RMS_EPS: EPST[:, 0:1], LN_EPS: EPST[:, 1:2]}

        GLUS = sb("GLUS", [128, 8, 16, 34], BF16)
        GLUSr = [Res()] * 8
        with contextlib.ExitStack() as es0:
            NXIN = 4
            XIN = [es0.enter_context(nc.sbuf_tensor(f"xin{i}", [128, D], F32)) for i in range(NXIN)]
            XINr = [Res() for _ in range(NXIN)]
            dma(SP, ncs[:, 0:26, :], sc.rearrange("(b r) d -> b r d", r=30)[:, 4:30, :], (), (), is_out=True)
            blocks = [(xp, i * 128, 128, i // 4, (i % 4) * 128) for i in range(16)] + [(xs, 0, 64, 4, 0)]
            for bi, (src, r0, n, t, c0) in enumerate(blocks):
                xi, xir = XIN[bi % NXIN], XINr[bi % NXIN]
                dma(SP, xi[:n, :], src[r0:r0 + n, :], (), (xir,))
                tok0 = TILES[t][0] + c0
                for half in range(2):
                    pb, pbr = bank()
                    fns = []
                    for j in range(4):
                        c = half * 4 + j
                        fns.append(lambda pb=pb, j=j, c=c, xi=xi, n=n: nc.tensor.transpose(
                            pb[:, j * 128:j * 128 + n], xi[:n, c * 128:(c + 1) * 128], IDENT[:n, :n]))
                    pe_multi(fns, (xir, IDENTr), (pbr,))
                    src_ap = pb[:].rearrange("p (j n) -> p j n", j=4)[:, :, :n]
                    dst_ap = XT[:, half * 4:half * 4 + 4, tok0:tok0 + n]
                    wr = [XTr[half * 4 + j][t] for j in range(4)]
                    if half == 0:
                        op(ACT, lambda d=dst_ap, s=src_ap: nc.scalar.copy(out=d, in_=s), (pbr,), wr)
                    else:
                        op(DVE, lambda d=dst_ap, s=src_ap: nc.vector.tensor_copy(out=d, in_=s), (pbr,), wr)
            for blk in range(4):
                xi, xir = XIN[(blk + 1) % NXIN], XINr[(blk + 1) % NXIN]
                dma(SP, xi[:120, :], sc[blk * 120:(blk + 1) * 120, :], (), (xir,))
                for half in range(2):
                    pb, pbr = bank()
                    fns = []
                    for j in range(4):
                        c = half * 4 + j
                        fns.append(lambda pb=pb, j=j, c=c, xi=xi: nc.tensor.transpose(
                            pb[:, j * 128:j * 128 + 120], xi[:120, c * 128:(c + 1) * 128], IDENT[:120, :120]))
                    pe_multi(fns, (xir, IDENTr), (pbr,))
                    for j in range(4):
                        c = half * 4 + j
                        s_ap = pb[:, j * 128:j * 128 + 120].rearrange("p (b r) -> p b r", b=4)
                        d_ap = GLUS[:, c, 4 * blk:4 * blk + 4, 0:30]
                        if j % 2 == 0:
                            op(ACT, lambda d=d_ap, s=s_ap: (nc.scalar.mul(out=d, in_=s, mul=1.0 / GSC) if USE_TANH else nc.scalar.copy(out=d, in_=s)), (pbr,), (GLUSr[c],))
                        else:
                            op(DVE, lambda d=d_ap, s=s_ap: (nc.vector.tensor_scalar_mul(out=d, in0=s, scalar1=1.0 / GSC) if USE_TANH else nc.vector.tensor_copy(out=d, in_=s)),
                               (pbr,), (GLUSr[c],))
            f0 = fence(XINr)

        def norm_s1(t):
            t0, n = TILES[t]
            xr = [XTr[c][t] for c in range(8)]
            op(ACT, lambda: nc.scalar.activation(out=HT[:, :, t0:t0 + n], in_=XT[:, :, t0:t0 + n], func=AF.Square),
               xr, [HTr[c][t] for c in range(8)])

        def rmsnorm_tile(t, gcol):
            norm_s1(t)
            norm_s2(t, gcol)

        def norm_s2(t, gcol):
            t0, n = TILES[t]
            pb, pbr = bank()
            mmgroup([(pb[:, :n], ONESM[:], HT[:, c, t0:t0 + n]) for c in range(8)],
                    [HTr[c][t] for c in range(8)] + [ONESMr], (pbr,))
            rsqrt_eps(RSTD[:, :n], RSTDr, pb[:, :n], (pbr,), RMS_EPS)
            for c in range(8):
                op(DVE, lambda c=c: nc.vector.scalar_tensor_tensor(
                    out=HT[:, c, t0:t0 + n], in0=XT[:, c, t0:t0 + n], scalar=vcol(gcol, c), in1=RSTD[:, :n],
                    op0=ALU.mult, op1=ALU.mult), (XTr[c][t], RSTDr, VECr), (HTr[c][t],))

        def transpose_out(src_fn, n, dst_dram, reads, oi):
            ot, otr = OUTTs[oi % len(OUTTs)], OUTTr[oi % len(OUTTs)]
            for half in range(2):
                pb, pbr = bank()
                fns = []
                for j in range(4):
                    c = half * 4 + j
                    fns.append(lambda pb=pb, j=j, c=c: nc.tensor.transpose(
                        pb[:n, j * 128:(j + 1) * 128], src_fn(c), IDENT[:, :]))
                pe_multi(fns, list(reads) + [IDENTr], (pbr,))
                if half == 0:
                    op(ACT, lambda pb=pb: nc.scalar.copy(out=ot[:n, 0:512], in_=pb[:n, :]), (pbr,), (otr,))
                else:
                    op(ACT, lambda pb=pb: nc.scalar.copy(out=ot[:n, 512:1024], in_=pb[:n, :]), (pbr,), (otr,))
            dma(SP, dst_dram, ot[:n, :], (otr,), (), is_out=True)

        ffn_norm_done = [set(), set()]

        with contextlib.ExitStack() as es1:
            def sb1(name, shape, dt):
                return es1.enter_context(nc.sbuf_tensor(name, list(shape), dt)), Res(f0)
            GA, _ = sb1("GA", [128, 8, 544], BF16)
            GAr = [Res(f0) for _ in range(8)] if PER_CHUNK_G else [Res(f0)] * 8
            SIG, SIGr = sb1("SIG", [128, 512], F32)
            CONV, _ = sb1("CONV", [128, 8, 512], F32)
            CONVr = [Res(f0) for _ in range(8)]
            CB, CBr = sb1("CB", [128, 512], BF16)
            CSQ, CSQr = sb1("CSQ", [128, 512], BF16)
            DG, _ = sb1("DG", [128, 2, KW, 128], BF16)
            DGr = [Res(f0), Res(f0)]
            MEAN, MEANr = sb1("MEAN", [128, 512], F32)
            RS2, RS2r = sb1("RS2", [128, 512], F32)
            G32, G32r = sb1("G32", [128, 8, 64], F32)
            SIG4, SIG4r = sb1("SIG4", [128, 64], F32)
            CONV4, _ = sb1("CONV4", [128, 8, 64], F32)
            CONV4r = [Res(f0) for _ in range(8)]
            CB4, CB4r = sb1("CB4", [128, 8, 64], BF16)
            CSQ4, CSQ4r = sb1("CSQ4", [128, 8, 64], BF16)
            MEAN4, MEAN4r = sb1("MEAN4", [128, 64], F32)
            RS24, RS24r = sb1("RS24", [128, 64], F32)
            G32b, G32br = sb1("G32b", [128, 8, 64], F32)
            OUTT, otr0 = sb1("OUTT", [128, D], F32)
            OUTTs.append(OUTT)
            OUTTr.append(otr0)
            m0_res = [SIGr, CBr, CSQr, MEANr, RS2r, G32r, otr0, SIG4r, CB4r, CSQ4r, MEAN4r, RS24r, G32br] \
                + CONVr + CONV4r + DGr + GAr
            op(POOL, lambda: nc.gpsimd.memset(GA[:, :, 0:30], 0.0), (), GAr)
            dg_state = {"i": 0}

            def m0_stats(t, c):
                n = TILES[t][1]
                pe_multi([lambda: nc.tensor.matmul(PS[0][:, :n], ONESM[:], CB[:, :n], start=(c == 0), stop=(c == 7))],
                         (CBr, ONESMr), (PSr[0],))
                pe_multi([lambda: nc.tensor.matmul(PS[1][:, :n], ONESM[:], CSQ[:, :n], start=(c == 0), stop=(c == 7))],
                         (CSQr, ONESMr), (PSr[1],))

            def m0_conv(t, c, G, Gr, dslot):
                t0, n = TILES[t]
                cv, cvr = bank()
                if t == 4:
                    mms = [(cv[:, :n], DG[:, dslot, k, :], G[:, c, :, k:k + 4]) for k in range(KW)]
                else:
                    mms = [(cv[:, :n], DG[:, dslot, k, :], G[:, c, k:k + n]) for k in range(KW)]
                mmgroup(mms, (DGr[dslot], Gr[c]), (cvr,))
                if DELAY_STATS and c > 0:
                    m0_stats(t, c - 1)
                op(ACT, lambda: nc.scalar.activation(out=CONV[:, c, :n], in_=cv[:, :n], func=AF.Identity,
                                                     bias=vcol(V_BDW, c), scale=GSC), (cvr, VECr), (CONVr[c],))
                op(ACT, lambda: nc.scalar.activation(out=CB[:, :n], in_=cv[:, :n], func=AF.Identity,
                                                     bias=vcol(V_BDW, c), scale=GSC), (cvr, VECr), (CBr,))
                op(ACT, lambda: nc.scalar.activation(out=CSQ[:, :n], in_=cv[:, :n], func=AF.Square,
                                                     bias=vcol(V_BDW, c), scale=GSC), (cvr, VECr), (CSQr,))
                if not DELAY_STATS:
                    m0_stats(t, c)

            def m0_conv_rider(c, dslot):
                cv, cvr = bank()
                mms = [(cv[:, :64], DG[:, dslot, k, :], GLUS[:, c, :, k:k + 4]) for k in range(KW)]
                mmgroup(mms, (DGr[dslot], GLUSr[c]), (cvr,))
                op(ACT, lambda: nc.scalar.activation(out=CONV4[:, c, :], in_=cv[:, :64], func=AF.Identity,
                                                     bias=vcol(V_BDW, c), scale=GSC), (cvr, VECr), (CONV4r[c],))
                op(ACT, lambda: nc.scalar.activation(out=CB4[:, c, :], in_=cv[:, :64], func=AF.Identity,
                                                     bias=vcol(V_BDW, c), scale=GSC), (cvr, VECr), (CB4r,))
                op(ACT, lambda: nc.scalar.activation(out=CSQ4[:, c, :], in_=cv[:, :64], func=AF.Square,
                                                     bias=vcol(V_BDW, c), scale=GSC), (cvr, VECr), (CSQ4r,))

            def m0_rider_pw1(c, blkA, blkG, cc):
                t0, n = TILES[4]
                hr = [HTr[k][4] for k in range(8)]
                a1, a1r = bank()
                a2, a2r = bank()
                mmgroup([(a1[:, :n], blkA[0][:, k, cc:cc + 128], HT[:, k, t0:t0 + n]) for k in range(8)],
                        hr + [blkA[1]], (a1r,))
                mmgroup([(a2[:, :n], blkG[0][:, k, cc:cc + 128], HT[:, k, t0:t0 + n]) for k in range(8)],
                        hr + [blkG[1]], (a2r,))
                op(ACT, lambda: nc.scalar.activation(out=SIG4[:, :], in_=a2[:, :n], func=AF.Sigmoid,
                                                     bias=vcol(V_B1G, c), scale=1.0), (a2r, VECr), (SIG4r,))
                op(DVE, lambda: nc.vector.scalar_tensor_tensor(
                    out=GLUS[:, c, :, 30:34], in0=a1[:, 0:64].rearrange("p (b t) -> p b t", b=16),
                    scalar=vcol(V_B1A, c), in1=SIG4[:, :].rearrange("p (b t) -> p b t", b=16),
                    op0=ALU.add, op1=ALU.mult), (a1r, SIG4r, VECr), (GLUSr[c],))
                op(DVE, lambda: nc.vector.scalar_tensor_tensor(
                    out=G32b[:, c, :], in0=a1[:, 0:64], scalar=vcol(V_B1A, c), in1=SIG4[:, :],
                    op0=ALU.add, op1=ALU.mult), (a1r, SIG4r, VECr), (G32br,))

            def m0_chunks(t):
                t0, n = TILES[t]
                sample = (t == 4)
                rider = (t == 3)
                if sample:
                    G, Gr = GLUS, GLUSr
                else:
                    G, Gr = GA, GAr
                    if t > 0:
                        op(POOL, lambda: nc.gpsimd.tensor_copy(out=GA[:, :, 0:30], in_=GA[:, :, 512:542]), (), GAr)
                hr = [HTr[k][t] for k in range(8)]
                blkA = blkG = None
                pend = None
                for c in range(8):
                    if c % 4 == 0:
                        if c == 4:
                            yield "mid"
                        blkA = next_block()
                        blkG = next_block()
                    cc = (c % 4) * 128
                    a1, a1r = bank()
                    a2, a2r = bank()
                    mmgroup([(a1[:, :n], blkA[0][:, k, cc:cc + 128], HT[:, k, t0:t0 + n]) for k in range(8)],
                            hr + [blkA[1]], (a1r,))
                    mmgroup([(a2[:, :n], blkG[0][:, k, cc:cc + 128], HT[:, k, t0:t0 + n]) for k in range(8)],
                            hr + [blkG[1]], (a2r,))
                    if USE_TANH:
                        op(ACT, lambda a2=a2, c=c: nc.scalar.activation(out=SIG[:, :n], in_=a2[:, :n], func=AF.Tanh,
                                                                         bias=HB[:, c:c + 1], scale=0.5),
                           (a2r, HBr), (SIGr,))
                        op(DVE, lambda a1=a1, c=c: nc.vector.scalar_tensor_tensor(
                            out=SIG[:, :n], in0=a1[:, :n], scalar=vcol(V_B1A, c), in1=SIG[:, :n], op0=ALU.add, op1=ALU.mult),
                           (a1r, VECr), (SIGr,))
                    else:
                        op(ACT, lambda a2=a2, c=c: nc.scalar.activation(out=SIG[:, :n], in_=a2[:, :n], func=AF.Sigmoid,
                                                                         bias=vcol(V_B1G, c), scale=1.0),
                           (a2r, VECr), (SIGr,))
                    glu_op1 = ALU.add if USE_TANH else ALU.mult
                    if sample:
                        g_out = G[:, c, :, 30:34]
                        a_in = a1[:, 0:64].rearrange("p (b t) -> p b t", b=16)
                        s_in = SIG[:, 0:64].rearrange("p (b t) -> p b t", b=16)
                    else:
                        g_out = G[:, c, 30:30 + n]
                        a_in = a1[:, :n]
                        s_in = SIG[:, :n]
                    op(DVE, lambda g_out=g_out, a_in=a_in, s_in=s_in, c=c: nc.vector.scalar_tensor_tensor(
                        out=g_out, in0=a_in, scalar=vcol(V_B1A, c), in1=s_in, op0=ALU.add, op1=glu_op1),
                       (a1r, SIGr, VECr), (Gr[c],))
                    if t == 3:
                        lo, cnt = (482, 30)
                        op(DVE, lambda a1=a1, c=c, lo=lo, cnt=cnt: nc.vector.scalar_tensor_tensor(
                            out=G32[:, c, 0:cnt], in0=a1[:, lo:lo + cnt], scalar=vcol(V_B1A, c),
                            in1=SIG[:, lo:lo + cnt], op0=ALU.add, op1=glu_op1), (a1r, SIGr, VECr), (G32r,))
                    if rider:
                        m0_rider_pw1(c, blkA, blkG, cc)
                    dslot = dg_state["i"] % 2
                    dg_state["i"] += 1
                    wsrc = VEC[:, V_WDW + c * KW:V_WDW + (c + 1) * KW].unsqueeze(2).broadcast_to([128, KW, 128])
                    isrc = IDB[:].unsqueeze(1).broadcast_to([128, KW, 128])
                    op(POOL, lambda dslot=dslot, wsrc=wsrc, isrc=isrc: nc.gpsimd.tensor_tensor(
                        out=DG[:, dslot, :, :], in0=isrc, in1=wsrc, op=ALU.mult), (IDBr, VECr), (DGr[dslot],))
                    if pend is not None:
                        m0_conv(t, pend[0], G, Gr, pend[1])
                        if rider:
                            m0_conv_rider(pend[0], pend[1])
                    pend = (c, dslot)
                    yield "step"
                m0_conv(t, pend[0], G, Gr, pend[1])
                if rider:
                    m0_conv_rider(pend[0], pend[1])
                if DELAY_STATS:
                    m0_stats(t, 7)

            def m0_L_pieces(t):
                if t == 4:
                    return m0_L_pieces_g(t, CONV4, CONV4r, MEAN4, MEAN4r, RS24, RS24r)
                return m0_L_pieces_g(t, CONV, CONVr, MEAN, MEANr, RS2, RS2r)

            def m0_L_pieces_g(t, CONV, CONVr, MEAN, MEANr, RS2, RS2r):
                t0, n = TILES[t]
                pieces = []

                def stats():
                    op(ACT, lambda: nc.scalar.copy(out=MEAN[:, :n], in_=PS[0][:, :n]), (PSr[0],), (MEANr,))
                    op(DVE, lambda: nc.vector.tensor_tensor(out=RS2[:, :n], in0=MEAN[:, :n], in1=MEAN[:, :n], op=ALU.mult),
                       (MEANr,), (RS2r,))
                    op(DVE, lambda: nc.vector.tensor_tensor(out=RS2[:, :n], in0=PS[1][:, :n], in1=RS2[:, :n], op=ALU.subtract),
                       (PSr[1],), (RS2r,))
                    rsqrt_eps(RS2[:, :n], RS2r, RS2[:, :n], (), LN_EPS)
                pieces.append(stats)
                for c in range(8):
                    def ln(c=c):
                        op(DVE, lambda: nc.vector.tensor_tensor(out=CONV[:, c, :n], in0=CONV[:, c, :n], in1=MEAN[:, :n],
                                                                op=ALU.subtract), (MEANr,), (CONVr[c],))
                        op(DVE, lambda: nc.vector.tensor_tensor(out=CONV[:, c, :n], in0=CONV[:, c, :n], in1=RS2[:, :n],
                                                                op=ALU.mult), (RS2r,), (CONVr[c],))
                        op(ACT, lambda: nc.scalar.activation(out=HT[:, c, t0:t0 + n], in_=CONV[:, c, :n], func=AF.Silu,
                                                             bias=vcol(V_LNB, c), scale=vcol(V_LNG, c)),
                           (CONVr[c], VECr), (HTr[c][t],))
                    pieces.append(ln)
                return pieces

            def m0_P(tiles):
                if isinstance(tiles, int):
                    tiles = [tiles]
                blkP = None
                for m in range(8):
                    if m % 4 == 0:
                        blkP = next_block()
                    mc = (m % 4) * 128
                    for t in tiles:
                        t0, n = TILES[t]
                        hr = [HTr[k][t] for k in range(8)]
                        ob, obr = bank()
                        mmgroup([(ob[:, :n], blkP[0][:, k, mc:mc + 128], HT[:, k, t0:t0 + n]) for k in range(8)],
                                hr + [blkP[1]], (obr,))
                        op(DVE, lambda ob=ob, m=m, t0=t0, n=n, t=t: nc.vector.scalar_tensor_tensor(
                            out=XT[:, m, t0:t0 + n], in0=ob[:, :n], scalar=vcol(V_BP2, m), in1=XT[:, m, t0:t0 + n],
                            op0=ALU.add, op1=ALU.add), (obr, VECr), (XTr[m][t],))
                for t in tiles:
                    ffn_norm_done[0].add(t)

            def m0_out_tail(cnt, write_fn):
                if USE_TANH:
                    op(DVE, lambda: nc.vector.tensor_scalar_mul(out=G32[:, :, 0:cnt], in0=G32[:, :, 0:cnt], scalar1=GSC),
                       (), (G32r,))
                write_fn()

            def m0_write_ncs():
                ot, otr = OUTTs[0], OUTTr[0]
                for half in range(2):
                    pb, pbr = bank()
                    fns = []
                    for j in range(4):
                        c = half * 4 + j
                        fns.append(lambda pb=pb, j=j, c=c: nc.tensor.transpose(
                            pb[:64, j * 128:(j + 1) * 128], G32b[:, c, 0:64], IDENT[:, :]))
                    pe_multi(fns, (G32br, IDENTr), (pbr,))
                    op(DVE, lambda pb=pb, half=half: nc.vector.tensor_copy(
                        out=ot[:64, half * 512:(half + 1) * 512], in_=pb[:64, :]), (pbr,), (otr,))
                for b in range(16):
                    dma(SP, ncs[b, 26:30, :], ot[4 * b:4 * b + 4, :], (otr,), (), is_out=True)

            rmsnorm_tile(0, V_CNG)
            pending = []
            jobs = []
            NP = NT - 1
            for t in range(NP):
                for ev in m0_chunks(t):
                    if ev == "step":
                        for _ in range(2):
                            if pending:
                                pending.pop(0)()
                        if jobs:
                            jobs.pop(0)()
                    elif ev == "mid":
                        while pending:
                            pending.pop(0)()
                        if t > 0:
                            m0_P(t - 1)
                            norm_s1(t - 1)
                            jobs.append(lambda t=t: (norm_s2(t - 1, V_FNG0), norm_s1(t + 1) if t + 1 < NP else None))
                            if t + 1 < NP:
                                jobs.append(lambda t=t: norm_s2(t + 1, V_CNG))
                                if t + 1 == 3:
                                    jobs.append(lambda: rmsnorm_tile(4, V_CNG))
                        elif t + 1 < NP:
                            norm_s1(t + 1)
                            jobs.append(lambda t=t: norm_s2(t + 1, V_CNG))
                while jobs:
                    jobs.pop(0)()
                pending = m0_L_pieces(t)
            m0_out_tail(30, lambda: transpose_out(lambda c: G32[:, c, 0:30], 30, ncp[:, :], (G32r,), 0))
            m0_out_tail(64, m0_write_ncs)
            pending.pop(0)()
            mmgroup([(PS[0][:, :64], ONESM[:], CB4[:, c, :]) for c in range(8)], (CB4r, ONESMr), (PSr[0],))
            mmgroup([(PS[1][:, :64], ONESM[:], CSQ4[:, c, :]) for c in range(8)], (CSQ4r, ONESMr), (PSr[1],))
            l4 = m0_L_pieces(4)
            l4.pop(0)()
            while pending or l4:
                if pending:
                    pending.pop(0)()
                if l4:
                    l4.pop(0)()
            m0_P([3, 4])
            rmsnorm_tile(3, V_FNG0)
            rmsnorm_tile(4, V_FNG0)
            f1 = fence(m0_res)
            OUTTs.clear()
            OUTTr.clear()
        pstate["all"] = True


        def ffn(l, f_in, gcol, tile_hook=None):
            with contextlib.ExitStack() as es2:
                HID = es2.enter_context(nc.sbuf_tensor(f"HID{l}", [128, 4, TT], BF16))
                HIDr = [[Res(f_in) for _ in range(NT)] for _ in range(4)]
                SG = [es2.enter_context(nc.sbuf_tensor(f"SG{l}_{i}", [128, 512], F32)) for i in range(2)]
                SGr = [Res(f_in), Res(f_in)]
                for t in range(NT):
                    if t not in ffn_norm_done[l]:
                        rmsnorm_tile(t, gcol)
                sgi = 0
                for s, (_c0, _ncol) in enumerate(FFN_SLICES):
                    nj = _ncol // 128
                    bg = next_block()
                    bu = next_block()
                    for j in range(nj):
                        for t in range(NT):
                            t0, n = TILES[t]
                            hr = [HTr[k][t] for k in range(8)]
                            gb, gbr = bank()
                            ub, ubr = bank()
                            mmgroup([(gb[:, :n], bg[0][:, k, j * 128:(j + 1) * 128], HT[:, k, t0:t0 + n]) for k in range(8)],
                                    hr + [bg[1]], (gbr,))
                            mmgroup([(ub[:, :n], bu[0][:, k, j * 128:(j + 1) * 128], HT[:, k, t0:t0 + n]) for k in range(8)],
                                    hr + [bu[1]], (ubr,))
                            sg, sgr = SG[sgi % 2], SGr[sgi % 2]
                            sgi += 1
                            op(ACT, lambda sg=sg, gb=gb, n=n: nc.scalar.activation(out=sg[:, :n], in_=gb[:, :n], func=AF.Silu),
                               (gbr,), (sgr,))
                            op(DVE, lambda sg=sg, ub=ub, n=n, j=j, t0=t0: nc.vector.tensor_tensor(
                                out=HID[:, j, t0:t0 + n], in0=ub[:, :n], in1=sg[:, :n], op=ALU.mult),
                               (ubr, sgr), (HIDr[j][t],))
                    bd = next_block()
                    for t in range(NT):
                        t0, n = TILES[t]
                        for m in range(8):
                            ob, obr = bank()
                            mmgroup([(ob[:, :n], bd[0][:, jj, m * 128:(m + 1) * 128], HID[:, jj, t0:t0 + n]) for jj in range(nj)],
                                    [HIDr[jj][t] for jj in range(nj)] + [bd[1]], (obr,))
                            op(DVE, lambda ob=ob, m=m, t0=t0, n=n: nc.vector.tensor_tensor(
                                out=XT[:, m, t0:t0 + n], in0=ob[:, :n], in1=XT[:, m, t0:t0 + n], op=ALU.add),
                               (obr,), (XTr[m][t],))
                        if tile_hook is not None and s == len(FFN_SLICES) - 1:
                            tile_hook(t)
                if tile_hook is not None:
                    tile_hook(NT)
                    tile_hook(NT + 1)
                res = SGr + [r for row in HIDr for r in row]
                return fence(res)

        def fmerge(a, b):
            out = dict(a)
            for k, tok in b.items():
                if k not in out or out[k][2] < tok[2]:
                    out[k] = tok
            return out

        esC = contextlib.ExitStack()
        es.enter_context(esC)

        def sbC(name, shape, dt):
            return esC.enter_context(nc.sbuf_tensor(name, list(shape), dt)), Res(f1)
        BCG, BCGr = sbC("BCG", [128, D], F32)
        BCB, BCBr = sbC("BCB", [128, D], F32)
        WM, WMr = sbC("WM", [128, 8, 128], BF16)
        BDB, BDBr = sbC("BDB", [64, 8, 64], BF16)
        BR, BRr = sbC("BR", [KB, 2, D], BF16)
        BRS, BRSr = sbC("BRS", [KB, 8, 64], BF16)
        esS = contextlib.ExitStack()
        esC.enter_context(esS)
        SCR = [(esS.enter_context(nc.sbuf_tensor(f"SCR{i}", [128, D], F32)), Res(f1)) for i in range(3)]
        SCB = (esS.enter_context(nc.sbuf_tensor("SCB", [128, D], BF16)), Res(f1))
        WMF, WMFr = SCR[2][0][:, :].rearrange("p (h t) -> p h t", h=8), SCR[2][1]
        BDF, BDFr = SCR[0][0][0:64, 0:512].rearrange("p (h t) -> p h t", h=8), SCR[0][1]
        if BIAS_K128:
            op(DVE, lambda: nc.vector.memset(BR[:], 0.0), (), (BRr,))
            op(DVE, lambda: nc.vector.memset(BRS[:], 0.0), (), (BRSr,))
        op(DVE, lambda: nc.vector.memset(BDF, 0.0), (), (BDFr,))
        dma(SP, BCG[:], rows_d[1:2, :].broadcast_to([128, D]), (), (BCGr,))
        dma(SP, BCB[:], rows_d[2:3, :].broadcast_to([128, D]), (), (BCBr,))
        dma(SP, WMF, wsT_d[:, :, :], (), (WMFr,))
        bdf_parts = []
        for b in range(16):
            rb = Res()
            rb.w = BDFr.w
            dma(SP, BDF[4 * b:4 * b + 4, :, 4 * b:4 * b + 4], wsT_d[0:4, :, 0:4], (), (rb,))
            bdf_parts.append(rb)
        Fs, Fr = SCR[1][0][0:2, :], SCR[1][1]
        Hs, Hr = SCB[0][0:2, :], SCB[1]
        dma(SP, Fs, bs_d[0:1, :].broadcast_to([2, D]), (), (Fr,))

        def consts_part2():
            op(DVE, lambda: nc.vector.tensor_tensor(out=WM[:], in0=WMF,
                                                    in1=TRI[:].unsqueeze(1).broadcast_to([128, 8, 128]), op=ALU.mult),
               (WMFr, TRIr), (WMr,))
            op(DVE, lambda: nc.vector.tensor_tensor(out=BDB[:], in0=BDF,
                                                    in1=TRI[0:64, 0:64].unsqueeze(1).broadcast_to([64, 8, 64]), op=ALU.mult),
               tuple(bdf_parts) + (TRIr,), (BDBr, BDFr))
            for idx, srcrow in ((0, None), (1, rows_d[0:1, :])):
                if srcrow is not None:
                    dma(SP, Fs, srcrow.broadcast_to([2, D]), (), (Fr,))
                op(DVE, lambda: nc.vector.tensor_copy(out=Hs, in_=Fs), (Fr,), (Hr,))
                op(DVE, lambda: nc.vector.tensor_tensor(out=Fs, in0=Fs, in1=Hs, op=ALU.subtract), (Hr,), (Fr,))
                op(DVE, lambda idx=idx: nc.vector.tensor_copy(out=BR[0:2, idx, :], in_=Fs), (Fr,), (BRr,))
                op(DVE, lambda idx=idx: nc.vector.tensor_copy(out=BR[0:1, idx, :], in_=Hs[0:1, :]), (Hr,), (BRr,))
            op(DVE, lambda: nc.vector.tensor_copy(
                out=BRS[0:2].rearrange("p h (b t) -> p h b t", b=16),
                in_=BR[0:2, 0, :].rearrange("p (h t) -> p h t", h=8)[:, :, 0:4].unsqueeze(2).broadcast_to([2, 8, 16, 4])),
               (BRr,), (BRSr,))

        def hook0(i):
            if i == 0:
                consts_part2()
            if i == 1:
                norm_s1(0)
                norm_s2(0, V_SNG)

        f2 = ffn(0, f1, V_FNG0, tile_hook=hook0)
        fS = fence([x[1] for x in SCR] + [SCB[1]] + bdf_parts)
        esS.close()
        f2 = fmerge(f2, fS)


        with contextlib.ExitStack() as es3:
            def sb3(name, shape, dt):
                return es3.enter_context(nc.sbuf_tensor(name, list(shape), dt)), Res(f2)
            U, _ = sb3("U", [128, 8, 512], F32)
            Ur = [Res(f2) for _ in range(8)]
            Vt = [sb3(f"V{i}", [128, D], F32) for i in range(4)]
            VBt = [sb3(f"VB{i}", [128, D], BF16) for i in range(5)]
            U4, _ = sb3("U4", [128, 8, 64], F32)
            U4r = [Res(f2) for _ in range(8)]
            STAT, _ = sb3("STAT", [128, 5, 2, 6], F32)
            MV, _ = sb3("MV", [128, 5, 2], F32)
            SDV, _ = sb3("SDV", [128, 5, 1], F32)
            STr = [Res(f2) for _ in range(5)]
            SDr = [Res(f2) for _ in range(5)]
            m1_res = [x[1] for x in Vt] + [x[1] for x in VBt] + Ur + U4r + STr + SDr + [BCGr, BCBr, WMr, BDBr, BRr, BRSr]

            vstate = {"v": 0}

            def m1_V(group):
                blkV = [next_block(), next_block()]
                for t in group:
                    m1_V_tile(t, blkV)

            def m1_V_tile(t, blkV):
                t0, n = TILES[t]
                sample = (t == 4)
                hr = [HTr[k][t] for k in range(8)]
                nblk = 1 if sample else 4
                nb = 64 if sample else 128
                done_blocks = []
                for ii in range(nblk):
                    i = 4 if sample else ii
                    b0 = t0 + ii * 128
                    V, Vr = Vt[vstate["v"] % 4]
                    vstate["v"] += 1
                    VB, VBr = VBt[i]
                    sr = STr[i]
                    for hf in range(2):
                        vb_, vbr_ = bank()
                        mms = [(vb_[:nb, :], HT[:, k, b0:b0 + nb], blkV[hf][0][:, k, :]) for k in range(8)]
                        mms.append((vb_[:nb, :], ONES2[:, :nb], BR[:, 1, hf * 512:(hf + 1) * 512]))
                        mmgroup(mms, hr + [blkV[hf][1], ONES2r, BRr], (vbr_,))
                        op(ACT, lambda vb_=vb_, hf=hf, V=V: nc.scalar.activation(
                            out=V[:nb, hf * 512:(hf + 1) * 512], in_=vb_[:nb, :], func=AF.Gelu), (vbr_,), (Vr,))
                    for hf in range(2):
                        op(DVE, lambda hf=hf, V=V, i=i: nc.vector.bn_stats(out=STAT[:nb, i, hf, :],
                                                                           in_=V[:nb, hf * 512:(hf + 1) * 512]),
                           (Vr,), (sr,))
                    op(DVE, lambda i=i: nc.vector.bn_aggr(out=MV[:nb, i, :],
                                                          in_=STAT[:nb, i, :, :].rearrange("p a b -> p (a b)")), (), (sr,))
                    op(DVE, lambda V=V, i=i: nc.vector.scalar_tensor_tensor(
                        out=V[:nb, :], in0=V[:nb, :], scalar=MV[:nb, i, 0:1], in1=BCG[:nb, :],
                        op0=ALU.subtract, op1=ALU.mult), (sr, BCGr), (Vr,))
                    if not sample:
                        done_blocks.append((i, V, Vr, VB, VBr))
                        continue
                    op(ACT, lambda i=i: nc.scalar.activation(out=SDV[:nb, i, :], in_=MV[:nb, i, 1:2], func=AF.Sqrt,
                                                             bias=EPS_AP[LN_EPS][:nb, :], scale=1.0), (EPSr, sr), (SDr[i],))
                    op(DVE, lambda i=i: nc.vector.reciprocal(out=SDV[:nb, i, :], in_=SDV[:nb, i, :]), (), (SDr[i],))
                    if sample:
                        op(DVE, lambda V=V, i=i: nc.vector.scalar_tensor_tensor(
                            out=V[:nb, :], in0=V[:nb, :], scalar=SDV[:nb, i, :], in1=BCB[:nb, :],
                            op0=ALU.mult, op1=ALU.add), (SDr[i], BCBr), (Vr,))
                        op(ACT, lambda V=V, VB=VB: nc.scalar.copy(out=VB[:nb, :], in_=V[:nb, :]), (Vr,), (VBr,))
                        dma(SP, nv_o[:, :], V[:64, :], (Vr,), (), is_out=True)
                if done_blocks:
                    op(ACT, lambda: nc.scalar.activation(out=SDV[:, 0:4, :], in_=MV[:, 0:4, 1:2], func=AF.Sqrt,
                                                         bias=EPS_AP[LN_EPS][:, :], scale=1.0),
                       [EPSr] + STr[0:4], SDr[0:4])
                    op(DVE, lambda: nc.vector.reciprocal(out=SDV[:, 0:4, :], in_=SDV[:, 0:4, :]), (), SDr[0:4])
                    for (i, V, Vr, VB, VBr) in done_blocks:
                        op(DVE, lambda V=V, VB=VB, i=i: nc.vector.scalar_tensor_tensor(
                            out=VB[:nb, :], in0=V[:nb, :], scalar=SDV[:nb, i, :], in1=BCB[:nb, :],
                            op0=ALU.mult, op1=ALU.add), (SDr[i], BCBr, Vr), (VBr,))

            def m1_U(group):
                blkU = None
                for c in range(8):
                    if c % 4 == 0:
                        blkU = next_block()
                    cc = (c % 4) * 128
                    for t in group:
                        t0, n = TILES[t]
                        hr = [HTr[k][t] for k in range(8)]
                        Ud, Udr = (U4, U4r) if t == 4 else (U, Ur)
                        ub, ubr = bank()
                        mmgroup([(ub[:, :n], blkU[0][:, k, cc:cc + 128], HT[:, k, t0:t0 + n]) for k in range(8)],
                                hr + [blkU[1]], (ubr,))
                        op(ACT, lambda ub=ub, c=c, Ud=Ud, n=n: nc.scalar.activation(out=Ud[:, c, :n], in_=ub[:, :n], func=AF.Gelu,
                                                                                   bias=vcol(V_BIU, c), scale=1.0),
                           (ubr, VECr), (Udr[c],))

            def m1_S(t):
                t0, n = TILES[t]
                sample = (t == 4)
                nblk = 1 if sample else 4
                nb = 64 if sample else 128
                Ud, Udr = (U4, U4r) if sample else (U, Ur)
                for i in range(nblk):
                    b0 = t0 + i * 128
                    VB, VBr = VBt[4 if sample else i]
                    for hh in range(2):
                        mb, mbr = bank()
                        fns = []
                        for hj in range(4):
                            h = hh * 4 + hj
                            o_ap = mb[:, hj * 128:hj * 128 + nb]
                            if sample:
                                w_ap, b_ap = BDB[:, h, :], BRS[:, h, :]
                            else:
                                w_ap, b_ap = WM[:, h, :], BR[:, 0, h * 128:(h + 1) * 128]
                            fns.append(lambda o_ap=o_ap, VB=VB, h=h, w_ap=w_ap: nc.tensor.matmul(
                                o_ap, VB[:nb, h * 128:(h + 1) * 128], w_ap, start=True, stop=False))
                            fns.append(lambda o_ap=o_ap, b_ap=b_ap: nc.tensor.matmul(
                                o_ap, ONES2[:, :], b_ap, start=False, stop=True))
                        pe_multi(fns, (VBr, WMr, BDBr, BRr, BRSr, ONES2r), (mbr,))
                        m_in = mb[:].rearrange("p (j n) -> p j n", j=4)[:, :, :nb]
                        u_in = Ud[:, hh * 4:hh * 4 + 4, i * 128:i * 128 + nb]
                        um_out = HT[:, hh * 4:hh * 4 + 4, b0:b0 + nb]
                        op(DVE, lambda m_in=m_in, u_in=u_in, um_out=um_out: nc.vector.tensor_tensor(
                            out=um_out, in0=m_in, in1=u_in, op=ALU.mult),
                           (mbr,) + tuple(Udr[hh * 4:hh * 4 + 4]), tuple(HTr[hh * 4 + q][t] for q in range(4)))

            def m1_O(group):
                blkO = None
                for m in range(8):
                    if m % 4 == 0:
                        blkO = next_block()
                    mc = (m % 4) * 128
                    for t in group:
                        t0, n = TILES[t]
                        hr = [HTr[k][t] for k in range(8)]
                        ob, obr = bank()
                        mmgroup([(ob[:, :n], blkO[0][:, k, mc:mc + 128], HT[:, k, t0:t0 + n]) for k in range(8)],
                                hr + [blkO[1]], (obr,))
                        op(DVE, lambda ob=ob, m=m, t0=t0, n=n, t=t: nc.vector.scalar_tensor_tensor(
                            out=XT[:, m, t0:t0 + n], in0=ob[:, :n], scalar=vcol(V_BOUT, m), in1=XT[:, m, t0:t0 + n],
                            op0=ALU.add, op1=ALU.add), (obr, VECr), (XTr[m][t],))
                for t in group:
                    ffn_norm_done[1].add(t)

            for gi, group in enumerate(M1_GROUPS):
                prevg = M1_GROUPS[gi - 1] if gi > 0 else []
                nextg = M1_GROUPS[gi + 1] if gi + 1 < len(M1_GROUPS) else []
                for x in nextg:
                    norm_s1(x)
                m1_V(group)
                for p in prevg:
                    norm_s2(p, V_FNG1)
                for x in nextg:
                    norm_s2(x, V_SNG)
                m1_U(group)
                for t in group:
                    m1_S(t)
                m1_O(group)
                for t in group:
                    norm_s1(t)
            for t in M1_GROUPS[-1]:
                norm_s2(t, V_FNG1)
            f3 = fence(m1_res)


        esC.close()
        with contextlib.ExitStack() as es4:
            YFs = [es4.enter_context(nc.sbuf_tensor(f"YF{i}", [128, 8, 512], F32)) for i in range(2)]
            for nm in ("OUTTa", "OUTTb"):
                OUTTs.append(es4.enter_context(nc.sbuf_tensor(nm, [128, D], F32)))
                OUTTr.append(Res(f3))
            YFrs = [[Res(f3) for _ in range(8)] for _ in range(2)]
            ostate = {"oi": 0}

            def final_A(t):
                t0, n = TILES[t]
                YF, YFr = YFs[t % 2], YFrs[t % 2]
                norm_s1(t)
                pb, pbr = bank()
                mmgroup([(pb[:, :n], ONESM[:], HT[:, c, t0:t0 + n]) for c in range(8)],
                        [HTr[c][t] for c in range(8)] + [ONESMr], (pbr,))
                rsqrt_eps(RSTD[:, :n], RSTDr, pb[:, :n], (pbr,), RMS_EPS)
                for c in range(8):
                    op(DVE, lambda c=c: nc.vector.scalar_tensor_tensor(
                        out=YF[:, c, :n], in0=XT[:, c, t0:t0 + n], scalar=vcol(V_FIN, c), in1=RSTD[:, :n],
                        op0=ALU.mult, op1=ALU.mult), (XTr[c][t], RSTDr, VECr), (YFr[c],))

            def final_B(t):
                t0, n = TILES[t]
                YF, YFr = YFs[t % 2], YFrs[t % 2]
                nblk = 1 if t == 4 else 4
                for i in range(nblk):
                    nb = 64 if t == 4 else 128
                    dst = y_s[:, :] if t == 4 else y_p[t0 + i * 128:t0 + (i + 1) * 128, :]
                    transpose_out(lambda c, i=i, nb=nb: YF[:, c, i * 128:i * 128 + nb], nb, dst, YFr, ostate["oi"])
                    ostate["oi"] += 1

            def hook1(i):
                if i >= NT:
                    if 0 <= i - 2 < NT:
                        final_B(i - 2)
                    if 0 <= i - 1 < NT:
                        final_A(i - 1)
                    return
                if 0 <= i - 1 < NT:
                    final_A(i - 1)
                if 0 <= i - 2 < NT:
                    final_B(i - 2)

            f4 = ffn(1, f3, V_FNG1, tile_hook=hook1)


        for tok in out_toks:
            SP.wait(tok)
    assert wst["next"] == len(sched), (wst, len(sched))
    return nc


def _prep_shared(inp):
    f = np.float32

    def pc(v):
        return np.ascontiguousarray(np.asarray(v, f).reshape(8, 128).T)

    cols = [pc(inp["conv_norm_g"][0]), pc(inp["conv_b_pw1"][0][:D]), pc(inp["conv_b_pw1"][0][D:]),
            pc(inp["conv_b_dw"][0]), pc(inp["conv_ln_g"][0]), pc(inp["conv_ln_b"][0]), pc(inp["conv_b_pw2"][0]),
            pc(inp["sgu_norm_g"][0]), pc(inp["sgu_b_in"][0][:D]), pc(inp["sgu_b_out"][0]),
            pc(inp["ffn_norm_g"][0]), pc(inp["ffn_norm_g"][1]), pc(inp["final_norm_g"])]
    wdw = np.asarray(inp["conv_w_dw"][0], f)
    wdw = wdw.T.reshape(8, 128, KW).transpose(1, 0, 2).reshape(128, 8 * KW)
    vecs = np.ascontiguousarray(np.concatenate(cols + [wdw], axis=1), dtype=f)
    assert vecs.shape == (128, NV)
    rows = np.ascontiguousarray(np.stack([inp["sgu_b_in"][0][D:], inp["sgu_ln_g"][0], inp["sgu_ln_b"][0]]), dtype=f)
    bs = np.ascontiguousarray(np.asarray(inp["sgu_b_s"][0], f).reshape(1, D))
    wsT = np.ascontiguousarray(np.asarray(inp["sgu_w_s"][0], f).transpose(2, 0, 1))
    return {
        "vecs": vecs, "rows": rows, "bs": bs, "wsT": wsT,
        "w_pw1": np.ascontiguousarray(inp["conv_w_pw1"][0], dtype=f),
        "w_pw2": np.ascontiguousarray(inp["conv_w_pw2"][0], dtype=f),
        "w_in": np.ascontiguousarray(inp["sgu_w_in"][0], dtype=f),
        "w_out": np.ascontiguousarray(inp["sgu_w_out"][0], dtype=f),
        "w_gate": np.ascontiguousarray(inp["ffn_w_gate"], dtype=f),
        "w_up": np.ascontiguousarray(inp["ffn_w_up"], dtype=f),
        "w_down": np.ascontiguousarray(inp["ffn_w_down"], dtype=f),
    }


_NC_CACHE = {}


def kernel(**inp):
    if "nc" not in _NC_CACHE:
        _NC_CACHE["nc"] = build()
    nc = _NC_CACHE["nc"]
    shared = _prep_shared(inp)
    x_prompt = np.asarray(inp["x_prompt"], np.float32)
    x_sample = np.asarray(inp["x_sample"], np.float32)
    state = np.asarray(inp["state_conv"], np.float32)
    in_maps = []
    for i in range(NCORES):
        m = dict(shared)
        m["xp"] = np.ascontiguousarray(x_prompt[i])
        m["xs"] = np.ascontiguousarray(x_sample[16 * i:16 * i + 16].reshape(64, D))
        m["sc"] = np.ascontiguousarray(state[0, 16 * i:16 * i + 16].reshape(480, D))
        in_maps.append(m)
    res = run_bass_kernel_spmd(nc, in_maps, core_ids=list(range(NCORES)))
    R = res.results
    y_prompt = np.stack([R[i]["y_p"] for i in range(NCORES)]).astype(np.float32)
    y_sample = np.concatenate([R[i]["y_s"].reshape(16, 4, D) for i in range(NCORES)]).astype(np.float32)
    ncp = np.stack([R[i]["ncp"] for i in range(NCORES)])[None].astype(np.float32)
    ncs = np.concatenate([R[i]["ncs"] for i in range(NCORES)])[None].astype(np.float32)
    nv = np.concatenate([R[i]["nv"].reshape(16, 4, D) for i in range(NCORES)])[None].astype(np.float32)
    return (y_prompt, y_sample, ncp, ncs, nv)
```

```python
import contextlib
import numpy as np
import concourse.bass as bass
import concourse.mybir as mybir
from concourse.bass_utils import run_bass_kernel_spmd

F32 = mybir.dt.float32
BF16 = mybir.dt.bfloat16
ALU = mybir.AluOpType
AF = mybir.ActivationFunctionType

NCORES = 8
D = 1024
DFF = 2816
TT = 2112
TILES = [(0, 512), (512, 512), (1024, 512), (1536, 512), (2048, 64)]
NT = len(TILES)
KW = 31
RMS_EPS = 1e-6
LN_EPS = 1e-5
USE_TANH = False
BIAS_K128 = True
GSC = 0.5 if USE_TANH else 1.0
PER_CHUNK_G = True
DELAY_STATS = True
SPLIT_NORM = True
USE_WSCRATCH = True

V_CNG, V_B1A, V_B1G, V_BDW, V_LNG, V_LNB, V_BP2 = 0, 8, 16, 24, 32, 40, 48
V_SNG, V_BIU, V_BOUT, V_FNG0, V_FNG1, V_FIN, V_WDW = 56, 64, 72, 80, 88, 96, 104
NV = V_WDW + 8 * KW
FFN_SLICES = [(2560, 256)] + [(512 * s, 512) for s in range(5)]


import bisect


class Res:
    __slots__ = ("w", "r")

    def __init__(self, init=None):
        self.w = None
        self.r = dict(init) if init else {}


def _tkey(tok):
    return id(tok[1])


def _tmax(a, b):
    return a if a[2] >= b[2] else b


class Q:
    def __init__(self, nc, eng, name):
        self.e = eng
        self.sem = nc.alloc_semaphore(name)
        self.instrs = []
        self.inc_pos = []
        self.seen_e = {}
        self.seen_d = {}

    def record(self, ins):
        self.instrs.append(ins)
        return ("E", self, len(self.instrs) - 1)

    def value_for(self, pos):
        if not self.inc_pos or pos > self.inc_pos[-1]:
            self.instrs[pos].then_inc(self.sem, 1)
            self.inc_pos.append(pos)
            return len(self.inc_pos), pos
        k = bisect.bisect_left(self.inc_pos, pos)
        return k + 1, self.inc_pos[k]

    def wait(self, tok):
        if tok[0] == "E":
            owner, pos = tok[1], tok[2]
            if self.seen_e.get(id(owner), -1) >= pos:
                return
            val, zpos = owner.value_for(pos)
            self.e.wait_ge(owner.sem, val)
            self.seen_e[id(owner)] = zpos
        else:
            sem, val = tok[1], tok[2]
            if self.seen_d.get(id(sem), 0) >= val:
                return
            self.e.wait_ge(sem, val)
            self.seen_d[id(sem)] = val


def _deps(q, reads, writes):
    need = {}
    for r in reads:
        if r.w is not None:
            k = _tkey(r.w)
            need[k] = _tmax(need[k], r.w) if k in need else r.w
    for w in writes:
        toks = list(w.r.values())
        if w.w is not None:
            toks.append(w.w)
        for tok in toks:
            k = _tkey(tok)
            need[k] = _tmax(need[k], tok) if k in need else tok
    for tok in need.values():
        q.wait(tok)


def _commit(tok, reads, writes):
    k = _tkey(tok)
    for r in reads:
        old = r.r.get(k)
        if old is None or old[2] < tok[2]:
            r.r[k] = tok
    for w in writes:
        w.w = tok
        w.r = {}


def build():
    nc = bass.Bass("TRN2", target_bir_lowering=False)

    def din(name, shape):
        return nc.dram_tensor(name, list(shape), F32, kind="ExternalInput").ap()

    def dout(name, shape):
        return nc.dram_tensor(name, list(shape), F32, kind="ExternalOutput").ap()

    xp = din("xp", [2048, D])
    xs = din("xs", [64, D])
    sc = din("sc", [480, D])
    vecs_d = din("vecs", [128, NV])
    rows_d = din("rows", [3, D])
    bs_d = din("bs", [1, D])
    wsT_d = din("wsT", [128, 8, 128])
    w_pw1 = din("w_pw1", [D, 2 * D])
    w_pw2 = din("w_pw2", [D, D])
    w_in = din("w_in", [D, 2 * D])
    w_out = din("w_out", [D, D])
    w_gate = din("w_gate", [2, D, DFF])
    w_up = din("w_up", [2, D, DFF])
    w_down = din("w_down", [2, DFF, D])
    y_p = dout("y_p", [2048, D])
    y_s = dout("y_s", [64, D])
    ncp = dout("ncp", [30, D])
    ncs = dout("ncs", [16, 30, D])
    nv_o = dout("nv", [64, D])
    wsc = {
        "w_pw1": nc.dram_tensor("wsc_pw1", [D, 2 * D], BF16, kind="Internal").ap(),
        "w_pw2": nc.dram_tensor("wsc_pw2", [D, D], BF16, kind="Internal").ap(),
        "w_in": nc.dram_tensor("wsc_in", [D, 2 * D], BF16, kind="Internal").ap(),
        "w_out": nc.dram_tensor("wsc_out", [D, D], BF16, kind="Internal").ap(),
    }
    wname = {id(w_pw1): "w_pw1", id(w_pw2): "w_pw2", id(w_in): "w_in", id(w_out): "w_out"}

    PE = Q(nc, nc.tensor, "s_pe")
    ACT = Q(nc, nc.scalar, "s_act")
    DVE = Q(nc, nc.vector, "s_dve")
    POOL = Q(nc, nc.gpsimd, "s_pool")
    SP = Q(nc, nc.sync, "s_sp")

    def op(q, fn, reads=(), writes=()):
        _deps(q, reads, writes)
        tok = q.record(fn())
        _commit(tok, reads, writes)
        return tok

    def pe_multi(fns, reads=(), writes=()):
        _deps(PE, reads, writes)
        ins = None
        for f in fns:
            ins = f()
        tok = PE.record(ins)
        _commit(tok, reads, writes)
        return tok

    def mmgroup(mms, reads=(), writes=()):
        n = len(mms)
        fns = []
        for i, (o, l, r) in enumerate(mms):
            fns.append(lambda o=o, l=l, r=r, i=i: nc.tensor.matmul(o, l, r, start=(i == 0), stop=(i == n - 1)))
        return pe_multi(fns, reads, writes)

    NDS = 16
    dsems = [[nc.alloc_semaphore(f"s_dma{i}"), 0] for i in range(NDS)]
    dstate = {"i": 0}
    out_toks = []

    def dma(q, out, in_, reads=(), writes=(), is_out=False):
        _deps(q, reads, writes)
        ent = dsems[dstate["i"] % NDS]
        dstate["i"] += 1
        if ent[1] > 0:
            q.wait(("D", ent[0], ent[1]))
        ins = q.e.dma_start(out=out, in_=in_)
        ent[1] += 16
        ins.then_inc(ent[0], 16)
        tok = ("D", ent[0], ent[1])
        _commit(tok, reads, writes)
        if is_out:
            out_toks.append(tok)
        return tok

    def fence(res_list):
        acc = {}
        for r in res_list:
            toks = list(r.r.values())
            if r.w is not None:
                toks.append(r.w)
            for tok in toks:
                k = _tkey(tok)
                if k not in acc or acc[k][2] < tok[2]:
                    acc[k] = tok
        return acc

    es = contextlib.ExitStack()
    with es:
        def sb(name, shape, dt):
            return es.enter_context(nc.sbuf_tensor(name, list(shape), dt))

        XT = sb("XT", [128, 8, TT], F32)
        HT = sb("HT", [128, 8, TT], BF16)
        NRING = 4
        RING = [sb(f"ring{i}", [128, 4096], BF16) for i in range(NRING)]
        RSTD = sb("RSTD", [128, 512], F32)
        VEC = sb("VEC", [128, NV], F32)
        IDENT = sb("IDENT", [128, 128], F32)
        ONESF = sb("ONESF", [128, 128], F32)
        IDB = sb("IDB", [128, 128], BF16)
        ONESM = sb("ONESM", [128, 128], BF16)
        KB = 128 if BIAS_K128 else 2
        ONES2 = sb("ONES2", [KB, 128], BF16)
        HB = sb("HB", [128, 8], F32)
        HBr = Res()
        PS = [es.enter_context(nc.psum_tensor(f"ps{i}", [128, 512], F32)) for i in range(8)]

        XTr = [[Res() for _ in range(NT)] for _ in range(8)]
        HTr = [[Res() for _ in range(NT)] for _ in range(8)]
        RINGr = [Res() for _ in range(NRING)]
        SQr, RSTDr, VECr, IDENTr, ONESFr, IDBr, ONESMr, ONES2r = (Res() for _ in range(8))
        OUTTr = []
        OUTTs = []
        PSr = [Res() for _ in range(8)]
        ring_sems = [[nc.alloc_semaphore(f"s_ring{i}"), 0] for i in range(NRING)]
        ring_sems_hw = [[nc.alloc_semaphore(f"s_ringhw{i}"), 0] for i in range(NRING)]

        pstate = {"i": 0, "all": False}

        def bank():
            i = (pstate["i"] % 8) if pstate["all"] else (2 + pstate["i"] % 6)
            pstate["i"] += 1
            return PS[i], PSr[i]

        def rsqrt_eps(out_ap, out_res, in_ap, in_reads, eps):
            op(ACT, lambda: nc.scalar.activation(out=out_ap, in_=in_ap, func=AF.Sqrt, bias=EPS_AP[eps][:out_ap.shape[0], :],
                                                 scale=1.0), tuple(in_reads) + (EPSr,), (out_res,))
            op(DVE, lambda: nc.vector.reciprocal(out=out_ap, in_=out_ap), (), (out_res,))

        def vcol(c0, c=0):
            return VEC[:, c0 + c:c0 + c + 1]

        sched = []
        for t in range(NT - 1):
            sched += [("col", w_pw1, None, 0), ("col", w_pw1, None, 1024)]
            if t > 0:
                sched += [("col", w_pw2, None, 0), ("col", w_pw2, None, 512)]
            sched += [("col", w_pw1, None, 512), ("col", w_pw1, None, 1536)]
        sched += [("col", w_pw2, None, 0), ("col", w_pw2, None, 512)]

        def ffn_sched(l):
            out = []
            for (c0, ncol) in FFN_SLICES:
                out += [("col", w_gate, l, c0, ncol), ("col", w_up, l, c0, ncol),
                        ("row", w_down, l, c0 // 128, ncol // 128)]
            return out

        sched += ffn_sched(0)
        M1_GROUPS = [[0], [1], [2], [3, 4]]
        for _g in M1_GROUPS:
            sched += [("col", w_in, None, 1024), ("col", w_in, None, 1536), ("col", w_in, None, 0),
                      ("col", w_in, None, 512), ("col", w_out, None, 0), ("col", w_out, None, 512)]
        sched += ffn_sched(1)
        wst = {"issued": 0, "next": 0}

        wsc_res = {}

        def issue_block(i):
            d = sched[i]
            slot = i % NRING
            if d[0] == "col" and d[2] is None and USE_WSCRATCH:
                w, c0 = d[1], d[3]
                key = (wname[id(w)], c0)
                dst = RING[slot][:, 0:4096].rearrange("p (k n) -> p k n", k=8)
                sc_ap = wsc[key[0]][:, c0:c0 + 512].rearrange("(k p) n -> p k n", p=128)
                r = RINGr[slot]
                ent = ring_sems[slot]
                if key not in wsc_res:
                    wsc_res[key] = Res()
                    _deps(POOL, (), (r,))
                    ins = nc.gpsimd.dma_start(out=dst, in_=w[:, c0:c0 + 512].rearrange("(k p) n -> p k n", p=128))
                    ent[1] += 16
                    ins.then_inc(ent[0], 16)
                    _commit(("D", ent[0], ent[1]), (), (r,))
                    dma(SP, sc_ap, dst, (r,), (wsc_res[key],))
                else:
                    ent = ring_sems_hw[slot]
                    _deps(SP, (wsc_res[key],), (r,))
                    ins = nc.sync.dma_start(out=dst, in_=sc_ap)
                    ent[1] += 16
                    ins.then_inc(ent[0], 16)
                    _commit(("D", ent[0], ent[1]), (wsc_res[key],), (r,))
                return
            if d[0] == "col":
                w, l, c0 = d[1], d[2], d[3]
                ncol = d[4] if len(d) > 4 else 512
                src = (w if l is None else w[l])[:, c0:c0 + ncol].rearrange("(k p) n -> p k n", p=128)
                dst = RING[slot][:, 0:8 * ncol].rearrange("p (k n) -> p k n", k=8)
            else:
                w, l, j0, nj = d[1], d[2], d[3], d[4]
                src = w[l][j0 * 128:(j0 + nj) * 128, :].rearrange("(j p) n -> p j n", p=128)
                dst = RING[slot][:, 0:nj * 1024].rearrange("p (j n) -> p j n", j=nj)
            r = RINGr[slot]
            _deps(POOL, (), (r,))
            ent = ring_sems[slot]
            ins = nc.gpsimd.dma_start(out=dst, in_=src)
            ent[1] += 16
            ins.then_inc(ent[0], 16)
            _commit(("D", ent[0], ent[1]), (), (r,))

        def next_block():
            i = wst["next"]
            wst["next"] += 1
            while wst["issued"] < min(len(sched), i + NRING - 1):
                issue_block(wst["issued"])
                wst["issued"] += 1
            d = sched[i]
            slot = i % NRING
            if d[0] == "col":
                ncol = d[4] if len(d) > 4 else 512
                view = RING[slot][:, 0:8 * ncol].rearrange("p (k n) -> p k n", k=8)
            else:
                nj = d[4]
                view = RING[slot][:, 0:nj * 1024].rearrange("p (j n) -> p j n", j=nj)
            return view, RINGr[slot]

        op(POOL, lambda: nc.gpsimd.memset(ONESF[:], 1.0), (), (ONESFr,))
        op(POOL, lambda: nc.gpsimd.affine_select(out=IDENT[:], in_=ONESF[:], pattern=[[1, 128]],
                                                 compare_op=ALU.is_equal, fill=0.0, base=0,
                                                 channel_multiplier=-1), (ONESFr,), (IDENTr,))
        op(POOL, lambda: nc.gpsimd.tensor_copy(out=IDB[:], in_=IDENT[:]), (IDENTr,), (IDBr,))
        TRI = sb("TRI", [128, 128], F32)
        TRIr = Res()
        op(POOL, lambda: nc.gpsimd.affine_select(out=TRI[:], in_=ONESF[:], pattern=[[1, 128]],
                                                 compare_op=ALU.is_ge, fill=0.0, base=0,
                                                 channel_multiplier=-1), (ONESFr,), (TRIr,))
        op(POOL, lambda: nc.gpsimd.memset(ONESM[:], 1.0 / 1024.0), (), (ONESMr,))
        op(POOL, lambda: nc.gpsimd.memset(ONES2[:], 1.0), (), (ONES2r,))
        dma(SP, VEC[:], vecs_d[:, :], (), (VECr,))
        op(DVE, lambda: nc.vector.tensor_scalar_mul(out=HB[:], in0=VEC[:, V_B1G:V_B1G + 8], scalar1=0.5), (VECr,), (HBr,))
        EPST = sb("EPST", [128, 2], F32)
        EPSr = Res()
        op(POOL, lambda: nc.gpsimd.memset(EPST[:, 0:1], RMS_EPS), (), (EPSr,))
        op(POOL, lambda: nc.gpsimd.memset(EPST[:, 1:2], LN_EPS), (), (EPSr,))
        EPS_AP = {RMS_EPS: EPST[:, 0:1], LN_EPS: EPST[:, 1:2]}

        GLUS = sb("GLUS", [128, 8, 16, 34], BF16)
        GLUSr = [Res()] * 8
        with contextlib.ExitStack() as es0:
            NXIN = 4
            XIN = [es0.enter_context(nc.sbuf_tensor(f"xin{i}", [128, D], F32)) for i in range(NXIN)]
            XINr = [Res() for _ in range(NXIN)]
            dma(SP, ncs[:, 0:26, :], sc.rearrange("(b r) d -> b r d", r=30)[:, 4:30, :], (), (), is_out=True)
            blocks = [(xp, i * 128, 128, i // 4, (i % 4) * 128) for i in range(16)] + [(xs, 0, 64, 4, 0)]
            for bi, (src, r0, n, t, c0) in enumerate(blocks):
                xi, xir = XIN[bi % NXIN], XINr[bi % NXIN]
                dma(SP, xi[:n, :], src[r0:r0 + n, :], (), (xir,))
                tok0 = TILES[t][0] + c0
                for half in range(2):
                    pb, pbr = bank()
                    fns = []
                    for j in range(4):
                        c = half * 4 + j
                        fns.append(lambda pb=pb, j=j, c=c, xi=xi, n=n: nc.tensor.transpose(
                            pb[:, j * 128:j * 128 + n], xi[:n, c * 128:(c + 1) * 128], IDENT[:n, :n]))
                    pe_multi(fns, (xir, IDENTr), (pbr,))
                    src_ap = pb[:].rearrange("p (j n) -> p j n", j=4)[:, :, :n]
                    dst_ap = XT[:, half * 4:half * 4 + 4, tok0:tok0 + n]
                    wr = [XTr[half * 4 + j][t] for j in range(4)]
                    if half == 0:
                        op(ACT, lambda d=dst_ap, s=src_ap: nc.scalar.copy(out=d, in_=s), (pbr,), wr)
                    else:
                        op(DVE, lambda d=dst_ap, s=src_ap: nc.vector.tensor_copy(out=d, in_=s), (pbr,), wr)
            for blk in range(4):
                xi, xir = XIN[(blk + 1) % NXIN], XINr[(blk + 1) % NXIN]
                dma(SP, xi[:120, :], sc[blk * 120:(blk + 1) * 120, :], (), (xir,))
                for half in range(2):
                    pb, pbr = bank()
                    fns = []
                    for j in range(4):
                        c = half * 4 + j
                        fns.append(lambda pb=pb, j=j, c=c, xi=xi: nc.tensor.transpose(
                            pb[:, j * 128:j * 128 + 120], xi[:120, c * 128:(c + 1) * 128], IDENT[:120, :120]))
                    pe_multi(fns, (xir, IDENTr), (pbr,))
                    for j in range(4):
                        c = half * 4 + j
                        s_ap = pb[:, j * 128:j * 128 + 120].rearrange("p (b r) -> p b r", b=4)
                        d_ap = GLUS[:, c, 4 * blk:4 * blk + 4, 0:30]
                        if j % 2 == 0:
                            op(ACT, lambda d=d_ap, s=s_ap: (nc.scalar.mul(out=d, in_=s, mul=1.0 / GSC) if USE_TANH else nc.scalar.copy(out=d, in_=s)), (pbr,), (GLUSr[c],))
                        else:
                            op(DVE, lambda d=d_ap, s=s_ap: (nc.vector.tensor_scalar_mul(out=d, in0=s, scalar1=1.0 / GSC) if USE_TANH else nc.vector.tensor_copy(out=d, in_=s)),
                               (pbr,), (GLUSr[c],))
            f0 = fence(XINr)

        def norm_s1(t):
            t0, n = TILES[t]
            xr = [XTr[c][t] for c in range(8)]
            if pstate["all"]:
                op(ACT, lambda: nc.scalar.activation(out=HT[:, 0:4, t0:t0 + n], in_=XT[:, 0:4, t0:t0 + n], func=AF.Square),
                   xr[0:4], [HTr[c][t] for c in range(4)])
                op(POOL, lambda: nc.gpsimd.tensor_tensor(out=HT[:, 4:8, t0:t0 + n], in0=XT[:, 4:8, t0:t0 + n],
                                                         in1=XT[:, 4:8, t0:t0 + n], op=ALU.mult),
                   xr[4:8], [HTr[c][t] for c in range(4, 8)])
                return
            op(ACT, lambda: nc.scalar.activation(out=HT[:, :, t0:t0 + n], in_=XT[:, :, t0:t0 + n], func=AF.Square),
               xr, [HTr[c][t] for c in range(8)])

        def rmsnorm_tile(t, gcol):
            norm_s1(t)
            norm_s2(t, gcol)

        def norm_s2(t, gcol):
            t0, n = TILES[t]
            pb, pbr = bank()
            mmgroup([(pb[:, :n], ONESM[:], HT[:, c, t0:t0 + n]) for c in range(8)],
                    [HTr[c][t] for c in range(8)] + [ONESMr], (pbr,))
            rsqrt_eps(RSTD[:, :n], RSTDr, pb[:, :n], (pbr,), RMS_EPS)
            for c in range(8):
                op(DVE, lambda c=c: nc.vector.scalar_tensor_tensor(
                    out=HT[:, c, t0:t0 + n], in0=XT[:, c, t0:t0 + n], scalar=vcol(gcol, c), in1=RSTD[:, :n],
                    op0=ALU.mult, op1=ALU.mult), (XTr[c][t], RSTDr, VECr), (HTr[c][t],))

        def transpose_out(src_fn, n, dst_dram, reads, oi):
            ot, otr = OUTTs[oi % len(OUTTs)], OUTTr[oi % len(OUTTs)]
            for half in range(2):
                pb, pbr = bank()
                fns = []
                for j in range(4):
                    c = half * 4 + j
                    fns.append(lambda pb=pb, j=j, c=c: nc.tensor.transpose(
                        pb[:n, j * 128:(j + 1) * 128], src_fn(c), IDENT[:, :]))
                pe_multi(fns, list(reads) + [IDENTr], (pbr,))
                if half == 0:
                    op(ACT, lambda pb=pb: nc.scalar.copy(out=ot[:n, 0:512], in_=pb[:n, :]), (pbr,), (otr,))
                else:
                    op(ACT, lambda pb=pb: nc.scalar.copy(out=ot[:n, 512:1024], in_=pb[:n, :]), (pbr,), (otr,))
            dma(SP, dst_dram, ot[:n, :], (otr,), (), is_out=True)

        ffn_norm_done = [set(), set()]

        with contextlib.ExitStack() as es1:
            def sb1(name, shape, dt):
                return es1.enter_context(nc.sbuf_tensor(name, list(shape), dt)), Res(f0)
            GA, _ = sb1("GA", [128, 8, 544], BF16)
            GAr = [Res(f0) for _ in range(8)] if PER_CHUNK_G else [Res(f0)] * 8
            SIG, SIGr = sb1("SIG", [128, 512], F32)
            CONV, _ = sb1("CONV", [128, 8, 512], F32)
            CONVr = [Res(f0) for _ in range(8)]
            CB, CBr = sb1("CB", [128, 512], BF16)
            CSQ, CSQr = sb1("CSQ", [128, 512], BF16)
            DG, _ = sb1("DG", [128, 2, KW, 128], BF16)
            DGr = [Res(f0), Res(f0)]
            MEAN, MEANr = sb1("MEAN", [128, 512], F32)
            RS2, RS2r = sb1("RS2", [128, 512], F32)
            G32, G32r = sb1("G32", [128, 8, 64], F32)
            SIG4, SIG4r = sb1("SIG4", [128, 64], F32)
            CONV4, _ = sb1("CONV4", [128, 8, 64], F32)
            CONV4r = [Res(f0) for _ in range(8)]
            CB4, CB4r = sb1("CB4", [128, 8, 64], BF16)
            CSQ4, CSQ4r = sb1("CSQ4", [128, 8, 64], BF16)
            MEAN4, MEAN4r = sb1("MEAN4", [128, 64], F32)
            RS24, RS24r = sb1("RS24", [128, 64], F32)
            G32b, G32br = sb1("G32b", [128, 8, 64], F32)
            OUTT, otr0 = sb1("OUTT", [128, D], F32)
            OUTTs.append(OUTT)
            OUTTr.append(otr0)
            m0_res = [SIGr, CBr, CSQr, MEANr, RS2r, G32r, otr0, SIG4r, CB4r, CSQ4r, MEAN4r, RS24r, G32br] \
                + CONVr + CONV4r + DGr + GAr
            op(POOL, lambda: nc.gpsimd.memset(GA[:, :, 0:30], 0.0), (), GAr)
            dg_state = {"i": 0}

            def m0_stats(t, c):
                n = TILES[t][1]
                pe_multi([lambda: nc.tensor.matmul(PS[0][:, :n], ONESM[:], CB[:, :n], start=(c == 0), stop=(c == 7))],
                         (CBr, ONESMr), (PSr[0],))
                pe_multi([lambda: nc.tensor.matmul(PS[1][:, :n], ONESM[:], CSQ[:, :n], start=(c == 0), stop=(c == 7))],
                         (CSQr, ONESMr), (PSr[1],))

            def m0_conv(t, c, G, Gr, dslot):
                t0, n = TILES[t]
                cv, cvr = bank()
                if t == 4:
                    mms = [(cv[:, :n], DG[:, dslot, k, :], G[:, c, :, k:k + 4]) for k in range(KW)]
                else:
                    mms = [(cv[:, :n], DG[:, dslot, k, :], G[:, c, k:k + n]) for k in range(KW)]
                mmgroup(mms, (DGr[dslot], Gr[c]), (cvr,))
                if DELAY_STATS and c > 0:
                    m0_stats(t, c - 1)
                op(ACT, lambda: nc.scalar.activation(out=CONV[:, c, :n], in_=cv[:, :n], func=AF.Identity,
                                                     bias=vcol(V_BDW, c), scale=GSC), (cvr, VECr), (CONVr[c],))
                op(ACT, lambda: nc.scalar.activation(out=CB[:, :n], in_=cv[:, :n], func=AF.Identity,
                                                     bias=vcol(V_BDW, c), scale=GSC), (cvr, VECr), (CBr,))
                op(ACT, lambda: nc.scalar.activation(out=CSQ[:, :n], in_=cv[:, :n], func=AF.Square,
                                                     bias=vcol(V_BDW, c), scale=GSC), (cvr, VECr), (CSQr,))
                if not DELAY_STATS:
                    m0_stats(t, c)

            def m0_conv_rider(c, dslot):
                cv, cvr = bank()
                mms = [(cv[:, :64], DG[:, dslot, k, :], GLUS[:, c, :, k:k + 4]) for k in range(KW)]
                mmgroup(mms, (DGr[dslot], GLUSr[c]), (cvr,))
                op(ACT, lambda: nc.scalar.activation(out=CONV4[:, c, :], in_=cv[:, :64], func=AF.Identity,
                                                     bias=vcol(V_BDW, c), scale=GSC), (cvr, VECr), (CONV4r[c],))
                op(ACT, lambda: nc.scalar.activation(out=CB4[:, c, :], in_=cv[:, :64], func=AF.Identity,
                                                     bias=vcol(V_BDW, c), scale=GSC), (cvr, VECr), (CB4r,))
                op(ACT, lambda: nc.scalar.activation(out=CSQ4[:, c, :], in_=cv[:, :64], func=AF.Square,
                                                     bias=vcol(V_BDW, c), scale=GSC), (cvr, VECr), (CSQ4r,))

            def m0_rider_pw1(c, blkA, blkG, cc):
                t0, n = TILES[4]
                hr = [HTr[k][4] for k in range(8)]
                a1, a1r = bank()
                a2, a2r = bank()
                mmgroup([(a1[:, :n], blkA[0][:, k, cc:cc + 128], HT[:, k, t0:t0 + n]) for k in range(8)],
                        hr + [blkA[1]], (a1r,))
                mmgroup([(a2[:, :n], blkG[0][:, k, cc:cc + 128], HT[:, k, t0:t0 + n]) for k in range(8)],
                        hr + [blkG[1]], (a2r,))
                op(ACT, lambda: nc.scalar.activation(out=SIG4[:, :], in_=a2[:, :n], func=AF.Sigmoid,
                                                     bias=vcol(V_B1G, c), scale=1.0), (a2r, VECr), (SIG4r,))
                op(DVE, lambda: nc.vector.scalar_tensor_tensor(
                    out=GLUS[:, c, :, 30:34], in0=a1[:, 0:64].rearrange("p (b t) -> p b t", b=16),
                    scalar=vcol(V_B1A, c), in1=SIG4[:, :].rearrange("p (b t) -> p b t", b=16),
                    op0=ALU.add, op1=ALU.mult), (a1r, SIG4r, VECr), (GLUSr[c],))
                op(DVE, lambda: nc.vector.scalar_tensor_tensor(
                    out=G32b[:, c, :], in0=a1[:, 0:64], scalar=vcol(V_B1A, c), in1=SIG4[:, :],
                    op0=ALU.add, op1=ALU.mult), (a1r, SIG4r, VECr), (G32br,))

            def m0_chunks(t):
                t0, n = TILES[t]
                sample = (t == 4)
                rider = (t == 3)
                if sample:
                    G, Gr = GLUS, GLUSr
                else:
                    G, Gr = GA, GAr
                    if t > 0:
                        op(POOL, lambda: nc.gpsimd.tensor_copy(out=GA[:, :, 0:30], in_=GA[:, :, 512:542]), (), GAr)
                hr = [HTr[k][t] for k in range(8)]
                blkA = blkG = None
                pend = None
                for c in range(8):
                    if c % 4 == 0:
                        if c == 4:
                            yield "mid"
                        blkA = next_block()
                        blkG = next_block()
                    cc = (c % 4) * 128
                    a1, a1r = bank()
                    a2, a2r = bank()
                    mmgroup([(a1[:, :n], blkA[0][:, k, cc:cc + 128], HT[:, k, t0:t0 + n]) for k in range(8)],
                            hr + [blkA[1]], (a1r,))
                    mmgroup([(a2[:, :n], blkG[0][:, k, cc:cc + 128], HT[:, k, t0:t0 + n]) for k in range(8)],
                            hr + [blkG[1]], (a2r,))
                    if USE_TANH:
                        op(ACT, lambda a2=a2, c=c: nc.scalar.activation(out=SIG[:, :n], in_=a2[:, :n], func=AF.Tanh,
                                                                         bias=HB[:, c:c + 1], scale=0.5),
                           (a2r, HBr), (SIGr,))
                        op(DVE, lambda a1=a1, c=c: nc.vector.scalar_tensor_tensor(
                            out=SIG[:, :n], in0=a1[:, :n], scalar=vcol(V_B1A, c), in1=SIG[:, :n], op0=ALU.add, op1=ALU.mult),
                           (a1r, VECr), (SIGr,))
                    else:
                        op(ACT, lambda a2=a2, c=c: nc.scalar.activation(out=SIG[:, :n], in_=a2[:, :n], func=AF.Sigmoid,
                                                                         bias=vcol(V_B1G, c), scale=1.0),
                           (a2r, VECr), (SIGr,))
                    glu_op1 = ALU.add if USE_TANH else ALU.mult
                    if sample:
                        g_out = G[:, c, :, 30:34]
                        a_in = a1[:, 0:64].rearrange("p (b t) -> p b t", b=16)
                        s_in = SIG[:, 0:64].rearrange("p (b t) -> p b t", b=16)
                    else:
                        g_out = G[:, c, 30:30 + n]
                        a_in = a1[:, :n]
                        s_in = SIG[:, :n]
                    op(DVE, lambda g_out=g_out, a_in=a_in, s_in=s_in, c=c: nc.vector.scalar_tensor_tensor(
                        out=g_out, in0=a_in, scalar=vcol(V_B1A, c), in1=s_in, op0=ALU.add, op1=glu_op1),
                       (a1r, SIGr, VECr), (Gr[c],))
                    if t == 3:
                        lo, cnt = (482, 30)
                        op(DVE, lambda a1=a1, c=c, lo=lo, cnt=cnt: nc.vector.scalar_tensor_tensor(
                            out=G32[:, c, 0:cnt], in0=a1[:, lo:lo + cnt], scalar=vcol(V_B1A, c),
                            in1=SIG[:, lo:lo + cnt], op0=ALU.add, op1=glu_op1), (a1r, SIGr, VECr), (G32r,))
                    if rider:
                        m0_rider_pw1(c, blkA, blkG, cc)
                    dslot = dg_state["i"] % 2
                    dg_state["i"] += 1
                    wsrc = VEC[:, V_WDW + c * KW:V_WDW + (c + 1) * KW].unsqueeze(2).broadcast_to([128, KW, 128])
                    isrc = IDB[:].unsqueeze(1).broadcast_to([128, KW, 128])
                    op(POOL, lambda dslot=dslot, wsrc=wsrc, isrc=isrc: nc.gpsimd.tensor_tensor(
                        out=DG[:, dslot, :, :], in0=isrc, in1=wsrc, op=ALU.mult), (IDBr, VECr), (DGr[dslot],))
                    if pend is not None:
                        m0_conv(t, pend[0], G, Gr, pend[1])
                        if rider:
                            m0_conv_rider(pend[0], pend[1])
                    pend = (c, dslot)
                    yield "step"
                m0_conv(t, pend[0], G, Gr, pend[1])
                if rider:
                    m0_conv_rider(pend[0], pend[1])
                if DELAY_STATS:
                    m0_stats(t, 7)

            def m0_L_pieces(t):
                if t == 4:
                    return m0_L_pieces_g(t, CONV4, CONV4r, MEAN4, MEAN4r, RS24, RS24r)
                return m0_L_pieces_g(t, CONV, CONVr, MEAN, MEANr, RS2, RS2r)

            def m0_L_pieces_g(t, CONV, CONVr, MEAN, MEANr, RS2, RS2r):
                t0, n = TILES[t]
                pieces = []

                def stats():
                    op(ACT, lambda: nc.scalar.copy(out=MEAN[:, :n], in_=PS[0][:, :n]), (PSr[0],), (MEANr,))
                    op(DVE, lambda: nc.vector.tensor_tensor(out=RS2[:, :n], in0=MEAN[:, :n], in1=MEAN[:, :n], op=ALU.mult),
                       (MEANr,), (RS2r,))
                    op(DVE, lambda: nc.vector.tensor_tensor(out=RS2[:, :n], in0=PS[1][:, :n], in1=RS2[:, :n], op=ALU.subtract),
                       (PSr[1],), (RS2r,))
                    rsqrt_eps(RS2[:, :n], RS2r, RS2[:, :n], (), LN_EPS)
                pieces.append(stats)
                for c in range(8):
                    def ln(c=c):
                        op(DVE, lambda: nc.vector.tensor_tensor(out=CONV[:, c, :n], in0=CONV[:, c, :n], in1=MEAN[:, :n],
                                                                op=ALU.subtract), (MEANr,), (CONVr[c],))
                        op(DVE, lambda: nc.vector.tensor_tensor(out=CONV[:, c, :n], in0=CONV[:, c, :n], in1=RS2[:, :n],
                                                                op=ALU.mult), (RS2r,), (CONVr[c],))
                        op(ACT, lambda: nc.scalar.activation(out=HT[:, c, t0:t0 + n], in_=CONV[:, c, :n], func=AF.Silu,
                                                             bias=vcol(V_LNB, c), scale=vcol(V_LNG, c)),
                           (CONVr[c], VECr), (HTr[c][t],))
                    pieces.append(ln)
                return pieces

            def m0_P(tiles):
                if isinstance(tiles, int):
                    tiles = [tiles]
                blkP = None
                for m in range(8):
                    if m % 4 == 0:
                        blkP = next_block()
                    mc = (m % 4) * 128
                    for t in tiles:
                        t0, n = TILES[t]
                        hr = [HTr[k][t] for k in range(8)]
                        ob, obr = bank()
                        mmgroup([(ob[:, :n], blkP[0][:, k, mc:mc + 128], HT[:, k, t0:t0 + n]) for k in range(8)],
                                hr + [blkP[1]], (obr,))
                        op(DVE, lambda ob=ob, m=m, t0=t0, n=n, t=t: nc.vector.scalar_tensor_tensor(
                            out=XT[:, m, t0:t0 + n], in0=ob[:, :n], scalar=vcol(V_BP2, m), in1=XT[:, m, t0:t0 + n],
                            op0=ALU.add, op1=ALU.add), (obr, VECr), (XTr[m][t],))
                for t in tiles:
                    ffn_norm_done[0].add(t)

            def m0_out_tail(cnt, write_fn):
                if USE_TANH:
                    op(DVE, lambda: nc.vector.tensor_scalar_mul(out=G32[:, :, 0:cnt], in0=G32[:, :, 0:cnt], scalar1=GSC),
                       (), (G32r,))
                write_fn()

            def m0_write_ncs():
                ot, otr = OUTTs[0], OUTTr[0]
                for half in range(2):
                    pb, pbr = bank()
                    fns = []
                    for j in range(4):
                        c = half * 4 + j
                        fns.append(lambda pb=pb, j=j, c=c: nc.tensor.transpose(
                            pb[:64, j * 128:(j + 1) * 128], G32b[:, c, 0:64], IDENT[:, :]))
                    pe_multi(fns, (G32br, IDENTr), (pbr,))
                    op(DVE, lambda pb=pb, half=half: nc.vector.tensor_copy(
                        out=ot[:64, half * 512:(half + 1) * 512], in_=pb[:64, :]), (pbr,), (otr,))
                for b in range(16):
                    dma(SP, ncs[b, 26:30, :], ot[4 * b:4 * b + 4, :], (otr,), (), is_out=True)

            rmsnorm_tile(0, V_CNG)
            pending = []
            jobs = []
            NP = NT - 1
            for t in range(NP):
                for ev in m0_chunks(t):
                    if ev == "step":
                        for _ in range(3):
                            if pending:
                                pending.pop(0)()
                        if jobs:
                            jobs.pop(0)()
                    elif ev == "mid":
                        while pending:
                            pending.pop(0)()
                        if t > 0:
                            m0_P(t - 1)
                            norm_s1(t - 1)
                            jobs.append(lambda t=t: (norm_s2(t - 1, V_FNG0), norm_s1(t + 1) if t + 1 < NP else None))
                            if t + 1 < NP:
                                jobs.append(lambda t=t: norm_s2(t + 1, V_CNG))
                                if t + 1 == 3:
                                    jobs.append(lambda: rmsnorm_tile(4, V_CNG))
                        elif t + 1 < NP:
                            norm_s1(t + 1)
                            jobs.append(lambda t=t: norm_s2(t + 1, V_CNG))
                while jobs:
                    jobs.pop(0)()
                pending = m0_L_pieces(t)
            m0_out_tail(30, lambda: transpose_out(lambda c: G32[:, c, 0:30], 30, ncp[:, :], (G32r,), 0))
            m0_out_tail(64, m0_write_ncs)
            pending.pop(0)()
            mmgroup([(PS[0][:, :64], ONESM[:], CB4[:, c, :]) for c in range(8)], (CB4r, ONESMr), (PSr[0],))
            mmgroup([(PS[1][:, :64], ONESM[:], CSQ4[:, c, :]) for c in range(8)], (CSQ4r, ONESMr), (PSr[1],))
            l4 = m0_L_pieces(4)
            l4.pop(0)()
            while pending or l4:
                if pending:
                    pending.pop(0)()
                if l4:
                    l4.pop(0)()
            m0_P([3, 4])
            rmsnorm_tile(3, V_FNG0)
            rmsnorm_tile(4, V_FNG0)
            f1 = fence(m0_res)
            OUTTs.clear()
            OUTTr.clear()
        pstate["all"] = True


        def ffn(l, f_in, gcol, tile_hook=None):
            with contextlib.ExitStack() as es2:
                HID = es2.enter_context(nc.sbuf_tensor(f"HID{l}", [128, 4, TT], BF16))
                HIDr = [[Res(f_in) for _ in range(NT)] for _ in range(4)]
                SG = [es2.enter_context(nc.sbuf_tensor(f"SG{l}_{i}", [128, 512], F32)) for i in range(2)]
                SGr = [Res(f_in), Res(f_in)]
                for t in range(NT):
                    if t not in ffn_norm_done[l]:
                        rmsnorm_tile(t, gcol)
                sgi = 0
                for s, (_c0, _ncol) in enumerate(FFN_SLICES):
                    nj = _ncol // 128
                    bg = next_block()
                    bu = next_block()
                    for j in range(nj):
                        for t in range(NT):
                            t0, n = TILES[t]
                            hr = [HTr[k][t] for k in range(8)]
                            gb, gbr = bank()
                            ub, ubr = bank()
                            mmgroup([(gb[:, :n], bg[0][:, k, j * 128:(j + 1) * 128], HT[:, k, t0:t0 + n]) for k in range(8)],
                                    hr + [bg[1]], (gbr,))
                            mmgroup([(ub[:, :n], bu[0][:, k, j * 128:(j + 1) * 128], HT[:, k, t0:t0 + n]) for k in range(8)],
                                    hr + [bu[1]], (ubr,))
                            sg, sgr = SG[sgi % 2], SGr[sgi % 2]
                            sgi += 1
                            op(ACT, lambda sg=sg, gb=gb, n=n: nc.scalar.activation(out=sg[:, :n], in_=gb[:, :n], func=AF.Silu),
                               (gbr,), (sgr,))
                            op(DVE, lambda sg=sg, ub=ub, n=n, j=j, t0=t0: nc.vector.tensor_tensor(
                                out=HID[:, j, t0:t0 + n], in0=ub[:, :n], in1=sg[:, :n], op=ALU.mult),
                               (ubr, sgr), (HIDr[j][t],))
                    bd = next_block()
                    for t in range(NT):
                        t0, n = TILES[t]
                        for m in range(8):
                            ob, obr = bank()
                            mmgroup([(ob[:, :n], bd[0][:, jj, m * 128:(m + 1) * 128], HID[:, jj, t0:t0 + n]) for jj in range(nj)],
                                    [HIDr[jj][t] for jj in range(nj)] + [bd[1]], (obr,))
                            op(DVE, lambda ob=ob, m=m, t0=t0, n=n: nc.vector.tensor_tensor(
                                out=XT[:, m, t0:t0 + n], in0=ob[:, :n], in1=XT[:, m, t0:t0 + n], op=ALU.add),
                               (obr,), (XTr[m][t],))
                        if tile_hook is not None and s == len(FFN_SLICES) - 1:
                            tile_hook(t)
                if tile_hook is not None:
                    tile_hook(NT)
                    tile_hook(NT + 1)
                res = SGr + [r for row in HIDr for r in row]
                return fence(res)

        def fmerge(a, b):
            out = dict(a)
            for k, tok in b.items():
                if k not in out or out[k][2] < tok[2]:
                    out[k] = tok
            return out

        esC = contextlib.ExitStack()
        es.enter_context(esC)

        def sbC(name, shape, dt):
            return esC.enter_context(nc.sbuf_tensor(name, list(shape), dt)), Res(f1)
        BCG, BCGr = sbC("BCG", [128, D], F32)
        BCB, BCBr = sbC("BCB", [128, D], F32)
        WM, WMr = sbC("WM", [128, 8, 128], BF16)
        BDB, BDBr = sbC("BDB", [64, 8, 64], BF16)
        BR, BRr = sbC("BR", [KB, 2, D], BF16)
        BRS, BRSr = sbC("BRS", [KB, 8, 64], BF16)
        esS = contextlib.ExitStack()
        esC.enter_context(esS)
        SCR = [(esS.enter_context(nc.sbuf_tensor(f"SCR{i}", [128, D], F32)), Res(f1)) for i in range(3)]
        SCB = (esS.enter_context(nc.sbuf_tensor("SCB", [128, D], BF16)), Res(f1))
        WMF, WMFr = SCR[2][0][:, :].rearrange("p (h t) -> p h t", h=8), SCR[2][1]
        BDF, BDFr = SCR[0][0][0:64, 0:512].rearrange("p (h t) -> p h t", h=8), SCR[0][1]
        if BIAS_K128:
            op(DVE, lambda: nc.vector.memset(BR[:], 0.0), (), (BRr,))
            op(DVE, lambda: nc.vector.memset(BRS[:], 0.0), (), (BRSr,))
        op(DVE, lambda: nc.vector.memset(BDF, 0.0), (), (BDFr,))
        dma(SP, BCG[:], rows_d[1:2, :].broadcast_to([128, D]), (), (BCGr,))
        dma(SP, BCB[:], rows_d[2:3, :].broadcast_to([128, D]), (), (BCBr,))
        dma(SP, WMF, wsT_d[:, :, :], (), (WMFr,))
        bdf_parts = []
        for b in range(16):
            rb = Res()
            rb.w = BDFr.w
            dma(SP, BDF[4 * b:4 * b + 4, :, 4 * b:4 * b + 4], wsT_d[0:4, :, 0:4], (), (rb,))
            bdf_parts.append(rb)
        Fs, Fr = SCR[1][0][0:2, :], SCR[1][1]
        Hs, Hr = SCB[0][0:2, :], SCB[1]
        dma(SP, Fs, bs_d[0:1, :].broadcast_to([2, D]), (), (Fr,))

        def consts_part2():
            op(DVE, lambda: nc.vector.tensor_tensor(out=WM[:], in0=WMF,
                                                    in1=TRI[:].unsqueeze(1).broadcast_to([128, 8, 128]), op=ALU.mult),
               (WMFr, TRIr), (WMr,))
            op(DVE, lambda: nc.vector.tensor_tensor(out=BDB[:], in0=BDF,
                                                    in1=TRI[0:64, 0:64].unsqueeze(1).broadcast_to([64, 8, 64]), op=ALU.mult),
               tuple(bdf_parts) + (TRIr,), (BDBr, BDFr))
            for idx, srcrow in ((0, None), (1, rows_d[0:1, :])):
                if srcrow is not None:
                    dma(SP, Fs, srcrow.broadcast_to([2, D]), (), (Fr,))
                op(DVE, lambda: nc.vector.tensor_copy(out=Hs, in_=Fs), (Fr,), (Hr,))
                op(DVE, lambda: nc.vector.tensor_tensor(out=Fs, in0=Fs, in1=Hs, op=ALU.subtract), (Hr,), (Fr,))
                op(DVE, lambda idx=idx: nc.vector.tensor_copy(out=BR[0:2, idx, :], in_=Fs), (Fr,), (BRr,))
                op(DVE, lambda idx=idx: nc.vector.tensor_copy(out=BR[0:1, idx, :], in_=Hs[0:1, :]), (Hr,), (BRr,))
            op(DVE, lambda: nc.vector.tensor_copy(
                out=BRS[0:2].rearrange("p h (b t) -> p h b t", b=16),
                in_=BR[0:2, 0, :].rearrange("p (h t) -> p h t", h=8)[:, :, 0:4].unsqueeze(2).broadcast_to([2, 8, 16, 4])),
               (BRr,), (BRSr,))

        def hook0(i):
            if i == 0:
                consts_part2()
            if i == 1:
                norm_s1(0)
                norm_s2(0, V_SNG)

        f2 = ffn(0, f1, V_FNG0, tile_hook=hook0)
        fS = fence([x[1] for x in SCR] + [SCB[1]] + bdf_parts)
        esS.close()
        f2 = fmerge(f2, fS)


        with contextlib.ExitStack() as es3:
            def sb3(name, shape, dt):
                return es3.enter_context(nc.sbuf_tensor(name, list(shape), dt)), Res(f2)
            U, _ = sb3("U", [128, 8, 512], F32)
            Ur = [Res(f2) for _ in range(8)]
            Vt = [sb3(f"V{i}", [128, D], F32) for i in range(4)]
            VBt = [sb3(f"VB{i}", [128, D], BF16) for i in range(5)]
            U4, _ = sb3("U4", [128, 8, 64], F32)
            U4r = [Res(f2) for _ in range(8)]
            STAT, _ = sb3("STAT", [128, 5, 2, 6], F32)
            MV, _ = sb3("MV", [128, 5, 2], F32)
            SDV, _ = sb3("SDV", [128, 5, 1], F32)
            STr = [Res(f2) for _ in range(5)]
            SDr = [Res(f2) for _ in range(5)]
            m1_res = [x[1] for x in Vt] + [x[1] for x in VBt] + Ur + U4r + STr + SDr + [BCGr, BCBr, WMr, BDBr, BRr, BRSr]

            vstate = {"v": 0}

            def m1_V(group):
                blkV = [next_block(), next_block()]
                for t in group:
                    m1_V_tile(t, blkV)

            def m1_V_tile(t, blkV):
                t0, n = TILES[t]
                sample = (t == 4)
                hr = [HTr[k][t] for k in range(8)]
                nblk = 1 if sample else 4
                nb = 64 if sample else 128
                done_blocks = []
                for ii in range(nblk):
                    i = 4 if sample else ii
                    b0 = t0 + ii * 128
                    V, Vr = Vt[vstate["v"] % 4]
                    vstate["v"] += 1
                    VB, VBr = VBt[i]
                    sr = STr[i]
                    for hf in range(2):
                        vb_, vbr_ = bank()
                        mms = [(vb_[:nb, :], HT[:, k, b0:b0 + nb], blkV[hf][0][:, k, :]) for k in range(8)]
                        mms.append((vb_[:nb, :], ONES2[:, :nb], BR[:, 1, hf * 512:(hf + 1) * 512]))
                        mmgroup(mms, hr + [blkV[hf][1], ONES2r, BRr], (vbr_,))
                        op(ACT, lambda vb_=vb_, hf=hf, V=V: nc.scalar.activation(
                            out=V[:nb, hf * 512:(hf + 1) * 512], in_=vb_[:nb, :], func=AF.Gelu), (vbr_,), (Vr,))
                    for hf in range(2):
                        op(DVE, lambda hf=hf, V=V, i=i: nc.vector.bn_stats(out=STAT[:nb, i, hf, :],
                                                                           in_=V[:nb, hf * 512:(hf + 1) * 512]),
                           (Vr,), (sr,))
                    op(DVE, lambda i=i: nc.vector.bn_aggr(out=MV[:nb, i, :],
                                                          in_=STAT[:nb, i, :, :].rearrange("p a b -> p (a b)")), (), (sr,))
                    op(DVE, lambda V=V, i=i: nc.vector.scalar_tensor_tensor(
                        out=V[:nb, :], in0=V[:nb, :], scalar=MV[:nb, i, 0:1], in1=BCG[:nb, :],
                        op0=ALU.subtract, op1=ALU.mult), (sr, BCGr), (Vr,))
                    if not sample:
                        done_blocks.append((i, V, Vr, VB, VBr))
                        continue
                    op(ACT, lambda i=i: nc.scalar.activation(out=SDV[:nb, i, :], in_=MV[:nb, i, 1:2], func=AF.Sqrt,
                                                             bias=EPS_AP[LN_EPS][:nb, :], scale=1.0), (EPSr, sr), (SDr[i],))
                    op(DVE, lambda i=i: nc.vector.reciprocal(out=SDV[:nb, i, :], in_=SDV[:nb, i, :]), (), (SDr[i],))
                    if sample:
                        op(DVE, lambda V=V, i=i: nc.vector.scalar_tensor_tensor(
                            out=V[:nb, :], in0=V[:nb, :], scalar=SDV[:nb, i, :], in1=BCB[:nb, :],
                            op0=ALU.mult, op1=ALU.add), (SDr[i], BCBr), (Vr,))
                        op(ACT, lambda V=V, VB=VB: nc.scalar.copy(out=VB[:nb, :], in_=V[:nb, :]), (Vr,), (VBr,))
                        dma(SP, nv_o[:, :], V[:64, :], (Vr,), (), is_out=True)
                if done_blocks:
                    op(ACT, lambda: nc.scalar.activation(out=SDV[:, 0:4, :], in_=MV[:, 0:4, 1:2], func=AF.Sqrt,
                                                         bias=EPS_AP[LN_EPS][:, :], scale=1.0),
                       [EPSr] + STr[0:4], SDr[0:4])
                    op(DVE, lambda: nc.vector.reciprocal(out=SDV[:, 0:4, :], in_=SDV[:, 0:4, :]), (), SDr[0:4])
                    for (i, V, Vr, VB, VBr) in done_blocks:
                        op(DVE, lambda V=V, VB=VB, i=i: nc.vector.scalar_tensor_tensor(
                            out=VB[:nb, :], in0=V[:nb, :], scalar=SDV[:nb, i, :], in1=BCB[:nb, :],
                            op0=ALU.mult, op1=ALU.add), (SDr[i], BCBr, Vr), (VBr,))

            def m1_U(group):
                blkU = None
                for c in range(8):
                    if c % 4 == 0:
                        blkU = next_block()
                    cc = (c % 4) * 128
                    for t in group:
                        t0, n = TILES[t]
                        hr = [HTr[k][t] for k in range(8)]
                        Ud, Udr = (U4, U4r) if t == 4 else (U, Ur)
                        ub, ubr = bank()
                        mmgroup([(ub[:, :n], blkU[0][:, k, cc:cc + 128], HT[:, k, t0:t0 + n]) for k in range(8)],
                                hr + [blkU[1]], (ubr,))
                        op(ACT, lambda ub=ub, c=c, Ud=Ud, n=n: nc.scalar.activation(out=Ud[:, c, :n], in_=ub[:, :n], func=AF.Gelu,
                                                                                   bias=vcol(V_BIU, c), scale=1.0),
                           (ubr, VECr), (Udr[c],))

            def m1_S(t):
                t0, n = TILES[t]
                sample = (t == 4)
                nblk = 1 if sample else 4
                nb = 64 if sample else 128
                Ud, Udr = (U4, U4r) if sample else (U, Ur)
                for i in range(nblk):
                    b0 = t0 + i * 128
                    VB, VBr = VBt[4 if sample else i]
                    for hh in range(2):
                        mb, mbr = bank()
                        fns = []
                        for hj in range(4):
                            h = hh * 4 + hj
                            o_ap = mb[:, hj * 128:hj * 128 + nb]
                            if sample:
                                w_ap, b_ap = BDB[:, h, :], BRS[:, h, :]
                            else:
                                w_ap, b_ap = WM[:, h, :], BR[:, 0, h * 128:(h + 1) * 128]
                            fns.append(lambda o_ap=o_ap, VB=VB, h=h, w_ap=w_ap: nc.tensor.matmul(
                                o_ap, VB[:nb, h * 128:(h + 1) * 128], w_ap, start=True, stop=False))
                            fns.append(lambda o_ap=o_ap, b_ap=b_ap: nc.tensor.matmul(
                                o_ap, ONES2[:, :], b_ap, start=False, stop=True))
                        pe_multi(fns, (VBr, WMr, BDBr, BRr, BRSr, ONES2r), (mbr,))
                        m_in = mb[:].rearrange("p (j n) -> p j n", j=4)[:, :, :nb]
                        u_in = Ud[:, hh * 4:hh * 4 + 4, i * 128:i * 128 + nb]
                        um_out = HT[:, hh * 4:hh * 4 + 4, b0:b0 + nb]
                        op(DVE, lambda m_in=m_in, u_in=u_in, um_out=um_out: nc.vector.tensor_tensor(
                            out=um_out, in0=m_in, in1=u_in, op=ALU.mult),
                           (mbr,) + tuple(Udr[hh * 4:hh * 4 + 4]), tuple(HTr[hh * 4 + q][t] for q in range(4)))

            def m1_O(group):
                blkO = None
                for m in range(8):
                    if m % 4 == 0:
                        blkO = next_block()
                    mc = (m % 4) * 128
                    for t in group:
                        t0, n = TILES[t]
                        hr = [HTr[k][t] for k in range(8)]
                        ob, obr = bank()
                        mmgroup([(ob[:, :n], blkO[0][:, k, mc:mc + 128], HT[:, k, t0:t0 + n]) for k in range(8)],
                                hr + [blkO[1]], (obr,))
                        op(DVE, lambda ob=ob, m=m, t0=t0, n=n, t=t: nc.vector.scalar_tensor_tensor(
                            out=XT[:, m, t0:t0 + n], in0=ob[:, :n], scalar=vcol(V_BOUT, m), in1=XT[:, m, t0:t0 + n],
                            op0=ALU.add, op1=ALU.add), (obr, VECr), (XTr[m][t],))
                for t in group:
                    ffn_norm_done[1].add(t)

            for gi, group in enumerate(M1_GROUPS):
                prevg = M1_GROUPS[gi - 1] if gi > 0 else []
                nextg = M1_GROUPS[gi + 1] if gi + 1 < len(M1_GROUPS) else []
                for x in nextg:
                    norm_s1(x)
                m1_V(group)
                for p in prevg:
                    norm_s2(p, V_FNG1)
                for x in nextg:
                    norm_s2(x, V_SNG)
                m1_U(group)
                for t in group:
                    m1_S(t)
                m1_O(group)
                for t in group:
                    norm_s1(t)
            for t in M1_GROUPS[-1]:
                norm_s2(t, V_FNG1)
            f3 = fence(m1_res)


        esC.close()
        with contextlib.ExitStack() as es4:
            YFs = [es4.enter_context(nc.sbuf_tensor(f"YF{i}", [128, 8, 512], F32)) for i in range(2)]
            for nm in ("OUTTa", "OUTTb"):
                OUTTs.append(es4.enter_context(nc.sbuf_tensor(nm, [128, D], F32)))
                OUTTr.append(Res(f3))
            YFrs = [[Res(f3) for _ in range(8)] for _ in range(2)]
            ostate = {"oi": 0}

            def final_A(t):
                t0, n = TILES[t]
                YF, YFr = YFs[t % 2], YFrs[t % 2]
                norm_s1(t)
                pb, pbr = bank()
                mmgroup([(pb[:, :n], ONESM[:], HT[:, c, t0:t0 + n]) for c in range(8)],
                        [HTr[c][t] for c in range(8)] + [ONESMr], (pbr,))
                rsqrt_eps(RSTD[:, :n], RSTDr, pb[:, :n], (pbr,), RMS_EPS)
                for c in range(8):
                    op(DVE, lambda c=c: nc.vector.scalar_tensor_tensor(
                        out=YF[:, c, :n], in0=XT[:, c, t0:t0 + n], scalar=vcol(V_FIN, c), in1=RSTD[:, :n],
                        op0=ALU.mult, op1=ALU.mult), (XTr[c][t], RSTDr, VECr), (YFr[c],))

            def final_B(t):
                t0, n = TILES[t]
                YF, YFr = YFs[t % 2], YFrs[t % 2]
                nblk = 1 if t == 4 else 4
                for i in range(nblk):
                    nb = 64 if t == 4 else 128
                    dst = y_s[:, :] if t == 4 else y_p[t0 + i * 128:t0 + (i + 1) * 128, :]
                    transpose_out(lambda c, i=i, nb=nb: YF[:, c, i * 128:i * 128 + nb], nb, dst, YFr, ostate["oi"])
                    ostate["oi"] += 1

            def hook1(i):
                if i >= NT:
                    if 0 <= i - 2 < NT:
                        final_B(i - 2)
                    if 0 <= i - 1 < NT:
                        final_A(i - 1)
                    return
                if 0 <= i - 1 < NT:
                    final_A(i - 1)
                if 0 <= i - 2 < NT:
                    final_B(i - 2)

            f4 = ffn(1, f3, V_FNG1, tile_hook=hook1)


        for tok in out_toks:
            SP.wait(tok)
    assert wst["next"] == len(sched), (wst, len(sched))
    return nc


def _prep_shared(inp):
    f = np.float32

    def pc(v):
        return np.ascontiguousarray(np.asarray(v, f).reshape(8, 128).T)

    cols = [pc(inp["conv_norm_g"][0]), pc(inp["conv_b_pw1"][0][:D]), pc(inp["conv_b_pw1"][0][D:]),
            pc(inp["conv_b_dw"][0]), pc(inp["conv_ln_g"][0]), pc(inp["conv_ln_b"][0]), pc(inp["conv_b_pw2"][0]),
            pc(inp["sgu_norm_g"][0]), pc(inp["sgu_b_in"][0][:D]), pc(inp["sgu_b_out"][0]),
            pc(inp["ffn_norm_g"][0]), pc(inp["ffn_norm_g"][1]), pc(inp["final_norm_g"])]
    wdw = np.asarray(inp["conv_w_dw"][0], f)
    wdw = wdw.T.reshape(8, 128, KW).transpose(1, 0, 2).reshape(128, 8 * KW)
    vecs = np.ascontiguousarray(np.concatenate(cols + [wdw], axis=1), dtype=f)
    assert vecs.shape == (128, NV)
    rows = np.ascontiguousarray(np.stack([inp["sgu_b_in"][0][D:], inp["sgu_ln_g"][0], inp["sgu_ln_b"][0]]), dtype=f)
    bs = np.ascontiguousarray(np.asarray(inp["sgu_b_s"][0], f).reshape(1, D))
    wsT = np.ascontiguousarray(np.asarray(inp["sgu_w_s"][0], f).transpose(2, 0, 1))
    return {
        "vecs": vecs, "rows": rows, "bs": bs, "wsT": wsT,
        "w_pw1": np.ascontiguousarray(inp["conv_w_pw1"][0], dtype=f),
        "w_pw2": np.ascontiguousarray(inp["conv_w_pw2"][0], dtype=f),
        "w_in": np.ascontiguousarray(inp["sgu_w_in"][0], dtype=f),
        "w_out": np.ascontiguousarray(inp["sgu_w_out"][0], dtype=f),
        "w_gate": np.ascontiguousarray(inp["ffn_w_gate"], dtype=f),
        "w_up": np.ascontiguousarray(inp["ffn_w_up"], dtype=f),
        "w_down": np.ascontiguousarray(inp["ffn_w_down"], dtype=f),
    }


_NC_CACHE = {}


def kernel(**inp):
    if "nc" not in _NC_CACHE:
        _NC_CACHE["nc"] = build()
    nc = _NC_CACHE["nc"]
    shared = _prep_shared(inp)
    x_prompt = np.asarray(inp["x_prompt"], np.float32)
    x_sample = np.asarray(inp["x_sample"], np.float32)
    state = np.asarray(inp["state_conv"], np.float32)
    in_maps = []
    for i in range(NCORES):
        m = dict(shared)
        m["xp"] = np.ascontiguousarray(x_prompt[i])
        m["xs"] = np.ascontiguousarray(x_sample[16 * i:16 * i + 16].reshape(64, D))
        m["sc"] = np.ascontiguousarray(state[0, 16 * i:16 * i + 16].reshape(480, D))
        in_maps.append(m)
    res = run_bass_kernel_spmd(nc, in_maps, core_ids=list(range(NCORES)))
    R = res.results
    y_prompt = np.stack([R[i]["y_p"] for i in range(NCORES)]).astype(np.float32)
    y_sample = np.concatenate([R[i]["y_s"].reshape(16, 4, D) for i in range(NCORES)]).astype(np.float32)
    ncp = np.stack([R[i]["ncp"] for i in range(NCORES)])[None].astype(np.float32)
    ncs = np.concatenate([R[i]["ncs"] for i in range(NCORES)])[None].astype(np.float32)
    nv = np.concatenate([R[i]["nv"].reshape(16, 4, D) for i in range(NCORES)])[None].astype(np.float32)
    return (y_prompt, y_sample, ncp, ncs, nv)
```
